# Optimizing a Trainium2 kernel written in Bass

```python
import jax, jax.numpy as jnp
from jax import lax
import numpy as np

D_MODEL = 1024
BATCH = 8
SEQ = 4096
DEPTH = 2

N_MIXERS = 2
N_RWKV_LAYERS = (DEPTH + 1) // 2
N_GMLP_LAYERS = DEPTH // 2
N_MEM = 256

RWKV_HEAD = 64
RWKV_HEADS = D_MODEL // RWKV_HEAD
LORA_DECAY = 64
LORA_AAA = 64
LORA_GATE = 160
GN_EPS = 64e-5

GMLP_CHUNK = 128
GMLP_WIDTH = 2 * D_MODEL
GMLP_GROUP_DIM = 128
GMLP_GROUPS = GMLP_WIDTH // GMLP_GROUP_DIM

XATTN_HEADS = 4
XATTN_HEAD_DIM = D_MODEL // XATTN_HEADS

D_FF = 2816
RMS_EPS = 1e-6

kernel_name = "rwkv7_gmlp_interleaved_macaron_memxattn"


def rms_norm(x, g):
    x32 = x.astype(jnp.float32)
    y = x32 * lax.rsqrt(jnp.mean(x32 * x32, axis=-1, keepdims=True) + RMS_EPS)
    return (y * g.astype(jnp.float32)).astype(x.dtype)


def swiglu_ffn(h, w_in, w_out):
    gate, up = jnp.split(h @ w_in, 2, axis=-1)
    return (jax.nn.silu(gate) * up) @ w_out


def token_shift(x):
    return jnp.pad(x[:, :-1], ((0, 0), (1, 0), (0, 0)))


def rwkv7_time_mix(h, mu, w_rkv, w0, w1, w2, a0, a1, a2, g1, g2, k_k, k_a, r_k, ln_w, ln_b, w_o):
    B, S, C = h.shape
    H, N = RWKV_HEADS, RWKV_HEAD
    dx = token_shift(h) - h
    xr = h + dx * mu[0]
    xw = h + dx * mu[1]
    xk = h + dx * mu[2]
    xv = h + dx * mu[3]
    xa = h + dx * mu[4]
    xg = h + dx * mu[5]
    x3 = jnp.stack([xr, xk, xv], axis=0)
    rkv = jnp.einsum('nbsc,cnd->nbsd', x3, w_rkv.reshape(C, 3, C))
    r, k, v = rkv[0], rkv[1], rkv[2]
    w = -jax.nn.softplus(-(w0 + jnp.tanh(xw @ w1) @ w2)) - 0.5
    decay = jnp.exp(-jnp.exp(w.astype(jnp.float32)))
    a = jax.nn.sigmoid(a0 + (xa @ a1) @ a2)
    g = jax.nn.sigmoid(xg @ g1) @ g2
    kk = (k * k_k).reshape(B, S, H, N).astype(jnp.float32)
    kk = kk / jnp.maximum(jnp.linalg.norm(kk, axis=-1, keepdims=True), 1e-12)
    k = k * (1.0 + (a - 1.0) * k_a)

    def heads(t):
        return t.reshape(B, S, H, N).astype(jnp.float32)

    r_h, k_h, v_h, a_h = heads(r), heads(k), heads(v), heads(a)
    b_h = kk * a_h
    xs = tuple(jnp.moveaxis(t, 1, 0) for t in (r_h, heads(decay), k_h, v_h, kk, b_h))

    def step(state, inp):
        r_t, w_t, k_t, v_t, kk_t, b_t = inp
        sa = jnp.einsum('bhij,bhj->bhi', state, -kk_t)
        state = (state * w_t[:, :, None, :] + sa[..., None] * b_t[:, :, None, :]
                 + v_t[..., :, None] * k_t[:, :, None, :])
        y = jnp.einsum('bhij,bhj->bhi', state, r_t)
        return state, y

    s0 = jnp.zeros((B, H, N, N), jnp.float32)
    _, ys = lax.scan(step, s0, xs)
    y = jnp.moveaxis(ys, 0, 1)
    mean = jnp.mean(y, axis=-1, keepdims=True)
    var = jnp.mean(jnp.square(y - mean), axis=-1, keepdims=True)
    y = ((y - mean) * lax.rsqrt(var + GN_EPS)).reshape(B, S, C)
    y = y * ln_w.astype(jnp.float32) + ln_b.astype(jnp.float32)
    bonus = jnp.sum(r_h * k_h * r_k.astype(jnp.float32), axis=-1, keepdims=True) * v_h
    y = (y + bonus.reshape(B, S, C)).astype(h.dtype)
    return (y * g) @ w_o


def gmlp_chunk_mix(h, w_uv, v_norm, w_s, b_s, w_o):
    B, S, _ = h.shape
    z = jax.nn.gelu(h @ w_uv)
    u, v = jnp.split(z, 2, axis=-1)
    v = rms_norm(v, v_norm)
    v = v.reshape(B, S // GMLP_CHUNK, GMLP_CHUNK, GMLP_GROUPS, GMLP_GROUP_DIM)
    causal = jnp.tril(jnp.ones((GMLP_CHUNK, GMLP_CHUNK), dtype=bool))
    ws = jnp.where(causal[None], w_s, jnp.zeros_like(w_s))
    sv = jnp.einsum('gts,bcsgd->bctgd', ws, v) + jnp.transpose(b_s)[:, :, None]
    sv = sv.reshape(B, S, GMLP_WIDTH)
    return (u * sv) @ w_o


def memory_cross_attention(h, mem_k, mem_v, wq, wo):
    B, S, C = h.shape
    q = (h @ wq).reshape(B, S, XATTN_HEADS, XATTN_HEAD_DIM)
    s = jnp.einsum('bshd,bmhd->bhsm', q, mem_k).astype(jnp.float32) * (XATTN_HEAD_DIM ** -0.5)
    p = jax.nn.softmax(s, axis=-1).astype(h.dtype)
    o = jnp.einsum('bhsm,bmhd->bshd', p, mem_v).reshape(B, S, C)
    return o @ wo


def setup_inputs(seed: int = 0) -> dict:
    key = jax.random.key(seed)
    ks = iter(jax.random.split(key, 48))
    C, F = D_MODEL, D_FF
    L, NA, NB = DEPTH, N_RWKV_LAYERS, N_GMLP_LAYERS

    def nrm(shape, scale):
        return jax.random.normal(next(ks), shape, jnp.float32) * scale

    def gain(shape):
        return 1.0 + 0.05 * jax.random.normal(next(ks), shape, jnp.float32)

    return {
        "x": jax.random.normal(next(ks), (BATCH, SEQ, C), jnp.float32),
        "mem": jax.random.normal(next(ks), (BATCH, N_MEM, C), jnp.float32),
        "mem_norm": gain((C,)),
        "mem_w_kv": nrm((C, 2 * C), C ** -0.5),
        "ffn1_norm": gain((L, C)),
        "ffn1_w_in": nrm((L, C, 2 * F), C ** -0.5),
        "ffn1_w_out": nrm((L, F, C), F ** -0.5),
        "mix_norm": gain((L, C)),
        "xattn_norm": gain((L, C)),
        "xattn_wq": nrm((L, C, C), C ** -0.5),
        "xattn_wo": nrm((L, C, C), C ** -0.5),
        "ffn2_norm": gain((L, C)),
        "ffn2_w_in": nrm((L, C, 2 * F), C ** -0.5),
        "ffn2_w_out": nrm((L, F, C), F ** -0.5),
        "rwkv_mu": jax.random.uniform(next(ks), (NA, 6, C), jnp.float32),
        "rwkv_w_rkv": nrm((NA, C, 3 * C), C ** -0.5),
        "rwkv_w0": jax.random.uniform(next(ks), (NA, C), jnp.float32, -6.0, -0.5),
        "rwkv_w1": nrm((NA, C, LORA_DECAY), C ** -0.5),
        "rwkv_w2": nrm((NA, LORA_DECAY, C), 0.5 * LORA_DECAY ** -0.5),
        "rwkv_a0": nrm((NA, C), 0.1),
        "rwkv_a1": nrm((NA, C, LORA_AAA), C ** -0.5),
        "rwkv_a2": nrm((NA, LORA_AAA, C), 0.5 * LORA_AAA ** -0.5),
        "rwkv_g1": nrm((NA, C, LORA_GATE), C ** -0.5),
        "rwkv_g2": nrm((NA, LORA_GATE, C), LORA_GATE ** -0.5),
        "rwkv_k_k": gain((NA, C)),
        "rwkv_k_a": gain((NA, C)),
        "rwkv_r_k": nrm((NA, RWKV_HEADS, RWKV_HEAD), 0.1),
        "rwkv_ln_w": gain((NA, C)),
        "rwkv_ln_b": nrm((NA, C), 0.01),
        "rwkv_w_o": nrm((NA, C, C), C ** -0.5),
        "gmlp_w_uv": nrm((NB, C, 2 * GMLP_WIDTH), C ** -0.5),
        "gmlp_v_norm": gain((NB, GMLP_WIDTH)),
        "gmlp_w_s": nrm((NB, GMLP_GROUPS, GMLP_CHUNK, GMLP_CHUNK), 0.5 * GMLP_CHUNK ** -0.5),
        "gmlp_b_s": gain((NB, GMLP_GROUPS, GMLP_CHUNK)),
        "gmlp_w_o": nrm((NB, GMLP_WIDTH, C), GMLP_WIDTH ** -0.5),
        "final_norm": gain((C,)),
    }


def reference(x, mem, mem_norm, mem_w_kv,
              ffn1_norm, ffn1_w_in, ffn1_w_out, mix_norm, xattn_norm, xattn_wq, xattn_wo,
              ffn2_norm, ffn2_w_in, ffn2_w_out,
              rwkv_mu, rwkv_w_rkv, rwkv_w0, rwkv_w1, rwkv_w2, rwkv_a0, rwkv_a1, rwkv_a2,
              rwkv_g1, rwkv_g2, rwkv_k_k, rwkv_k_a, rwkv_r_k, rwkv_ln_w, rwkv_ln_b, rwkv_w_o,
              gmlp_w_uv, gmlp_v_norm, gmlp_w_s, gmlp_b_s, gmlp_w_o,
              final_norm):
    B = mem.shape[0]
    mem_k, mem_v = jnp.split(rms_norm(mem, mem_norm) @ mem_w_kv, 2, axis=-1)
    mem_k = mem_k.reshape(B, N_MEM, XATTN_HEADS, XATTN_HEAD_DIM)
    mem_v = mem_v.reshape(B, N_MEM, XATTN_HEADS, XATTN_HEAD_DIM)

    for i in range(DEPTH):
        x = x + 0.5 * swiglu_ffn(rms_norm(x, ffn1_norm[i]), ffn1_w_in[i], ffn1_w_out[i])
        h = rms_norm(x, mix_norm[i])
        j = i // N_MIXERS
        if i % N_MIXERS == 0:
            y = rwkv7_time_mix(h, rwkv_mu[j], rwkv_w_rkv[j], rwkv_w0[j], rwkv_w1[j], rwkv_w2[j],
                               rwkv_a0[j], rwkv_a1[j], rwkv_a2[j], rwkv_g1[j], rwkv_g2[j],
                               rwkv_k_k[j], rwkv_k_a[j], rwkv_r_k[j], rwkv_ln_w[j], rwkv_ln_b[j],
                               rwkv_w_o[j])
        else:
            y = gmlp_chunk_mix(h, gmlp_w_uv[j], gmlp_v_norm[j], gmlp_w_s[j], gmlp_b_s[j], gmlp_w_o[j])
        x = x + y
        x = x + memory_cross_attention(rms_norm(x, xattn_norm[i]), mem_k, mem_v,
                                       xattn_wq[i], xattn_wo[i])
        x = x + 0.5 * swiglu_ffn(rms_norm(x, ffn2_norm[i]), ffn2_w_in[i], ffn2_w_out[i])
    return rms_norm(x, final_norm)
```

```python
import numpy as np
from contextlib import ExitStack
import concourse.bass as bass
import concourse.mybir as mybir
from concourse.bass_utils import run_bass_kernel_spmd

F32 = mybir.dt.float32
BF16 = mybir.dt.bfloat16
AF = mybir.ActivationFunctionType
ALU = mybir.AluOpType
AX = mybir.AxisListType

C = 1024
KC = C // 128
DFF = 2816
NMEM = 256
RMS_EPS = 1e-6
GN_EPS = 64e-5
SEM_EPOCH = 2000


class Tok:
    __slots__ = ("eng", "sem", "val", "needed", "is_dma", "idx")

    def __init__(self, eng, is_dma=False):
        self.eng = eng
        self.sem = None
        self.val = None
        self.needed = False
        self.is_dma = is_dma


class Buf:
    __slots__ = ("name", "writers", "readers")

    def __init__(self, name):
        self.name = name
        self.writers = {}
        self.readers = {}


class Sched:
    ENGS = ("pe", "act", "dve", "pool", "sp")

    def __init__(self, nc, stack):
        self.nc = nc
        self.stack = stack
        self.ops = {e: [] for e in self.ENGS}
        self.dma_sems = {}
        self.nsem = 0

    def new_sem(self, name):
        self.nsem += 1
        return self.stack.enter_context(self.nc.semaphore(name))

    def _deps(self, eng, reads, writes, is_dma):
        deps = []
        for b in reads:
            for k, t in b.writers.items():
                deps.append(t)
        for b in writes:
            for k, t in b.writers.items():
                if t.eng == eng and not t.is_dma and not is_dma and eng == "pe":
                    continue
                deps.append(t)
            for k, t in b.readers.items():
                if t.eng == eng and not t.is_dma and not is_dma and eng == "pe":
                    continue
                deps.append(t)
        return deps

    def _commit(self, tok, key, reads, writes):
        for b in reads:
            b.readers[key] = tok
        for b in writes:
            b.writers = {key: tok}
            b.readers = {}

    def op(self, eng, fn, reads=(), writes=()):
        deps = self._deps(eng, reads, writes, False)
        tok = Tok(eng)
        for d in deps:
            if d.eng == eng and not d.is_dma and eng == "pe":
                continue
            d.needed = True
        self.ops[eng].append((fn, deps, tok))
        self._commit(tok, eng, reads, writes)
        return tok

    def dma(self, queue, semname, fn, reads=(), writes=()):
        if semname not in self.dma_sems:
            self.dma_sems[semname] = [self.new_sem("d_" + semname), 0, None]
        s = self.dma_sems[semname]
        deps = self._deps(queue, reads, writes, True)
        if s[2] is not None:
            deps.append(s[2])
        for d in deps:
            d.needed = True
        tok = Tok(queue, is_dma=True)
        s[1] += 16
        tok.sem = s[0]
        tok.val = s[1]
        s[2] = tok
        self.ops[queue].append((fn, deps, tok))
        self._commit(tok, "dma:" + semname, reads, writes)
        return tok

    def finalize(self):
        for e in self.ENGS:
            cnt = 0
            sem = None
            for fn, deps, tok in self.ops[e]:
                if tok.is_dma or not tok.needed:
                    continue
                if sem is None or cnt == SEM_EPOCH:
                    sem = self.new_sem("e_%s_%d" % (e, self.nsem))
                    cnt = 0
                cnt += 1
                tok.sem = sem
                tok.val = cnt

    def replay(self, eng, engine):
        waited = {}
        for fn, deps, tok in self.ops[eng]:
            for d in deps:
                if d.eng == eng and not d.is_dma and eng == "pe":
                    continue
                key = id(d.sem)
                if waited.get(key, 0) >= d.val:
                    continue
                engine.wait_ge(d.sem, d.val)
                waited[key] = d.val
            ins = fn(engine)
            if tok.is_dma:
                ins.then_inc(tok.sem, 16)
            elif tok.needed:
                ins.then_inc(tok.sem, 1)

    def run(self, final_toks):
        self.finalize()
        nc = self.nc
        with nc.Block() as block:
            @block.tensor
            def _(e):
                self.replay("pe", e)

            @block.scalar
            def _(e):
                self.replay("act", e)

            @block.vector
            def _(e):
                self.replay("dve", e)

            @block.gpsimd
            def _(e):
                self.replay("pool", e)

            @block.sync
            def _(e):
                self.replay("sp", e)
                for t in final_toks:
                    e.wait_ge(t.sem, t.val)


RING_PAGES = 70
PAGE = 1024
A32_PAGES = 20
PAGE32 = 256
G = 512
NT = G // 128
LCH = 64
NCH = G // LCH
DEC = 0.6065306597126334

COLS = {}
_c = 0
for _name, _n in [("ffn1_norm0", 8), ("ffn1_norm1", 8), ("mix_norm0", 8), ("mix_norm1", 8),
                  ("xattn_norm0", 8), ("xattn_norm1", 8), ("ffn2_norm0", 8), ("ffn2_norm1", 8),
                  ("mem_norm", 8), ("final_norm", 8), ("mu", 48), ("w0", 8), ("a0", 8),
                  ("k_k", 8), ("k_a", 8), ("r_k", 8), ("gv_norm", 16), ("omka", 8)]:
    COLS[_name] = _c
    _c += _n
NCOLS = _c


class T:
    __slots__ = ("ap", "b")

    def __init__(self, ap, bufs):
        self.ap = ap
        self.b = list(bufs)


def _shape_view(ap2d, shape):
    if len(shape) == 1:
        return ap2d
    if len(shape) == 2:
        return ap2d.rearrange("p (a b) -> p a b", a=shape[0])
    if len(shape) == 3:
        return ap2d.rearrange("p (a b c) -> p a b c", a=shape[0], b=shape[1])
    raise ValueError(shape)


class WView:
    def __init__(self, view, kc, kstep):
        self.v = view
        self.kc = kc
        self.kstep = kstep
        self.piece_bufs = []

    def b(self, k):
        return self.piece_bufs[k // self.kstep]

    def all(self):
        return [b for pb in self.piece_bufs for b in pb]


class Prog:
    def __init__(self, S=4096):
        self.S = S
        self.NG = S // G
        self.nc = bass.Bass("TRN2", target_bir_lowering=False)
        self.stack = ExitStack()
        self.sch = Sched(self.nc, self.stack)
        self.bufs = {}
        self.dram = {}
        self.ring_off = 0
        self.ring_used = 0
        self.a32_off = 0
        self.wrr = 0
        self.gi = 0
        self.final_toks = []

    def din(self, name, shape, dt=F32):
        t = self.nc.dram_tensor(name, list(shape), dt, kind="ExternalInput").ap()
        self.dram[name] = t
        return t

    def dout(self, name, shape, dt=F32):
        return self.nc.dram_tensor(name, list(shape), dt, kind="ExternalOutput").ap()

    def dscratch(self, name, shape, dt=F32):
        return self.nc.dram_tensor(name, list(shape), dt, kind="Internal").ap()

    def sb(self, name, shape, dt):
        return self.stack.enter_context(self.nc.sbuf_tensor(name, list(shape), dt))

    def ps(self, name, shape, dt=F32):
        return self.stack.enter_context(self.nc.psum_tensor(name, list(shape), dt))

    def buf(self, name):
        if name not in self.bufs:
            self.bufs[name] = Buf(name)
        return self.bufs[name]

    def fixed(self, name, shape, dt):
        return T(self.sb(name, shape, dt), [self.buf(name)])

    def phase_begin(self, reset=False):
        self.ring_used = 0
        self.a32_off = 0
        if reset:
            self.ring_off = 0

    def _ring(self, nelem):
        npg = (nelem + PAGE - 1) // PAGE
        p0 = self.ring_off
        if p0 + npg > RING_PAGES:
            self.ring_used += RING_PAGES - p0
            p0 = 0
        self.ring_off = p0 + npg
        self.ring_used += npg
        assert self.ring_used <= RING_PAGES, "phase over-subscribes the ring arena"
        return p0 * PAGE, [self.buf("pg%d" % p) for p in range(p0, p0 + npg)]

    def r16(self, shape, part=128):
        n = int(np.prod(shape))
        off, pgs = self._ring(n)
        return T(_shape_view(self.ring[0:part, off:off + n], shape), pgs)

    def a32(self, shape, part=128):
        n = int(np.prod(shape))
        npg = (n + PAGE32 - 1) // PAGE32
        p0 = self.a32_off
        self.a32_off += npg
        assert self.a32_off <= A32_PAGES, "fp32 scratch over-subscribed"
        return T(_shape_view(self.arena32[0:part, p0 * PAGE32:p0 * PAGE32 + n], shape),
                 [self.buf("fp%d" % p) for p in range(p0, p0 + npg)])

    def wload(self, src_ap, kc, n, part=128, kstep=4):
        if kc * n <= PAGE * 4:
            kstep = kc
        while kstep < kc and (kstep * n) % PAGE != 0:
            kstep += 1
        off, pgs = self._ring(kc * n)
        view = self.ring[0:part, off:off + kc * n].rearrange("p (k n) -> p k n", k=kc)
        w = WView(view, kc, kstep)
        for k0 in range(0, kc, kstep):
            k1 = min(kc, k0 + kstep)
            pb = pgs[(k0 * n) // PAGE:((k1 * n) + PAGE - 1) // PAGE]
            w.piece_bufs.append(pb)
            for n0 in range(0, n, 1024):
                n1 = min(n, n0 + 1024)
                self.wrr = (self.wrr + 1) % 12
                self.sch.dma("pool", "w%d" % self.wrr,
                             (lambda e, o=view[:, k0:k1, n0:n1], i=src_ap[:, k0:k1, n0:n1]:
                              e.dma_start(out=o, in_=i)),
                             reads=(), writes=pb)
        return w

    def setup(self):
        nc, sch = self.nc, self.sch
        self.ring = self.sb("ring", [128, RING_PAGES * PAGE], BF16)
        self.arena32 = self.sb("arena32", [128, A32_PAGES * PAGE32], F32)
        self.cols = self.sb("cols_sb", [128, NCOLS], F32)
        self.bcols = self.buf("cols")
        self.ident_f = self.sb("ident_f", [128, 128], F32)
        self.ident = self.sb("ident", [128, 128], BF16)
        self.bident = self.buf("ident")
        self.xt = [self.fixed("xt%d" % i, [128, NT, C], F32) for i in range(2)]
        self.ss = self.fixed("ss", [128, NT], F32)
        self.rs = self.fixed("rs", [128, NT], F32)
        self.KT = self.fixed("KT", [128, KC, NMEM], BF16)
        self.Vm = self.fixed("Vm", [128, 2, C], BF16)
        self.ones = self.fixed("ones", [128, 128], BF16)
        self.bank = [T(self.ps("bank%d" % i, [128, 512], F32), [self.buf("bank%d" % i)]) for i in range(6)]
        self.bankT = [T(self.ps("bankT%d" % i, [128, 1024], BF16), [self.buf("bankT%d" % i)]) for i in range(2)]
        cols_d = self.din("cols", [128, NCOLS])
        sch.dma("sp", "cols", lambda e: e.dma_start(out=self.cols[:], in_=cols_d[:, :]), writes=[self.bcols])
        bif = self.buf("ident_f")
        sch.op("pool", lambda e: e.memset(self.ident_f[:], 1.0), writes=[bif])
        sch.op("pool", lambda e: e.affine_select(out=self.ident_f[:], in_=self.ident_f[:],
                                                 pattern=[[-1, 128]], compare_op=ALU.is_equal,
                                                 fill=0.0, base=0, channel_multiplier=1),
               reads=[bif], writes=[bif])
        sch.op("dve", lambda e: e.tensor_copy(out=self.ident[:], in_=self.ident_f[:]),
               reads=[bif], writes=[self.bident])
        sch.op("pool", lambda e: e.memset(self.ones.ap[:], 1.0), writes=self.ones.b)
        ka, om = COLS["k_a"], COLS["omka"]
        sch.op("dve", lambda e: e.tensor_scalar(out=self.cols[:, om:om + 8], in0=self.cols[:, ka:ka + 8],
                                                scalar1=-1.0, scalar2=1.0, op0=ALU.mult, op1=ALU.add),
               reads=[self.bcols], writes=[self.bcols])

    def dregion(self, dname, g):
        return self.buf("dr_%s_%d" % (dname, g))

    def load_x(self, src, sname, g, slot):
        view = src[g * G:(g + 1) * G, :].rearrange("(t p) c -> p t c", p=128)
        return self.sch.dma("sp", "xld%d" % slot,
                            lambda e, o=self.xt[slot].ap[:], i=view: e.dma_start(out=o, in_=i),
                            reads=[self.dregion(sname, g)], writes=self.xt[slot].b)

    def store_x(self, dst, dname, g, slot):
        view = dst[g * G:(g + 1) * G, :].rearrange("(t p) c -> p t c", p=128)
        return self.sch.dma("sp", "xst%d" % slot,
                            lambda e, o=view, i=self.xt[slot].ap[:]: e.dma_start(out=o, in_=i),
                            reads=self.xt[slot].b, writes=[self.dregion(dname, g)])

    def norm_T(self, slot, hT, xn, sq, gain_col, eps=RMS_EPS, nt=NT, col0=0):
        sch = self.sch
        xt = self.xt[slot]
        ss, rs = self.ss, self.rs
        for t in range(nt):
            sch.op("act", lambda e, t=t: e.activation(out=sq.ap[:], in_=xt.ap[:, t, :], func=AF.Square,
                                                      accum_out=ss.ap[:, t:t + 1]),
                   reads=xt.b, writes=sq.b + ss.b)
        sch.op("act", lambda e: e.activation(out=rs.ap[:, 0:nt], in_=ss.ap[:, 0:nt], func=AF.Sqrt,
                                             bias=float(eps), scale=1.0 / C), reads=ss.b, writes=rs.b)
        sch.op("dve", lambda e: e.reciprocal(out=rs.ap[:, 0:nt], in_=rs.ap[:, 0:nt]), reads=rs.b, writes=rs.b)
        for t in range(nt):
            sch.op("dve", lambda e, t=t: e.tensor_scalar(out=xn.ap[:, t, :], in0=xt.ap[:, t, :],
                                                         scalar1=rs.ap[:, t:t + 1], scalar2=None, op0=ALU.mult),
                   reads=xt.b + rs.b, writes=xn.b)
        for c in range(KC):
            bT = self.bankT[c % 2]
            for t in range(nt):
                sch.op("pe", lambda e, c=c, t=t, bT=bT: e.transpose(
                    out=bT.ap[:, t * 128:(t + 1) * 128], in_=xn.ap[:, t, c * 128:(c + 1) * 128],
                    identity=self.ident[:]), reads=xn.b + [self.bident], writes=bT.b)
            sch.op("act", lambda e, c=c, bT=bT: e.activation(
                out=hT.ap[:, c, col0:col0 + nt * 128], in_=bT.ap[:, 0:nt * 128], func=AF.Copy,
                scale=self.cols[:, gain_col + c:gain_col + c + 1]),
                reads=bT.b + [self.bcols], writes=hT.b)

    def out_proj(self, act, nk, W, scale, slot):
        sch = self.sch
        xt = self.xt[slot]
        for t in range(NT):
            for nb in range(2):
                bk = self.bank[4 + (t * 2 + nb) % 2]
                for k in range(nk):
                    sch.op("pe", lambda e, k=k, t=t, nb=nb, bk=bk: e.matmul(
                        bk.ap[:], lhsT=act.ap[:, k, t * 128:(t + 1) * 128],
                        rhs=W.v[:, k, nb * 512:(nb + 1) * 512], start=(k == 0), stop=(k == nk - 1)),
                        reads=act.b + W.b(k), writes=bk.b)
                sch.op("dve", lambda e, t=t, nb=nb, bk=bk: e.scalar_tensor_tensor(
                    out=xt.ap[:, t, nb * 512:(nb + 1) * 512], in0=bk.ap[:], scalar=float(scale),
                    in1=xt.ap[:, t, nb * 512:(nb + 1) * 512], op0=ALU.mult, op1=ALU.add),
                    reads=bk.b + xt.b, writes=xt.b)

    def ffn_pass(self, w_in, w_out, l, j0, nj, base, bname, dst, dname, gain_col, first):
        sch = self.sch
        self.phase_begin()
        win = w_in[l].rearrange("(kc p) n -> p kc n", p=128)
        wgs, wus = [], []
        for jb in range(0, nj, 2):
            nb_ = min(2, nj - jb)
            wgs.append(self.wload(win[:, :, (j0 + jb) * 128:(j0 + jb + nb_) * 128], KC, nb_ * 128, kstep=KC))
            wus.append(self.wload(win[:, :, DFF + (j0 + jb) * 128:DFF + (j0 + jb + nb_) * 128], KC, nb_ * 128, kstep=KC))
        wo = self.wload(w_out[l].rearrange("(kc p) n -> p kc n", p=128)[:, j0:j0 + nj, :], nj, C)
        hTs = [self.r16([KC, G]) for _ in range(2)]
        hids = [self.r16([nj, G]) for _ in range(2)]
        sgs = [self.r16([G]) for _ in range(2)]
        if first:
            xn = self.r16([NT, C])
            sq = self.r16([C])
        hsave = self.hsave
        gi0 = self.gi

        def prefetch(g):
            slot = (gi0 + g) % 2
            self.load_x(base, bname, g, slot)
            if not first:
                sch.dma("sp", "hld%d" % slot,
                        lambda e, o=hTs[slot].ap[:], i=hsave[g]: e.dma_start(out=o, in_=i),
                        reads=[self.dregion("hsave", g)], writes=hTs[slot].b)

        prefetch(0)
        tok = None
        for g in range(self.NG):
            slot = (gi0 + g) % 2
            hT, hid = hTs[slot], hids[slot]
            if g + 1 < self.NG:
                prefetch(g + 1)
            if first:
                self.norm_T(slot, hT, xn, sq, gain_col)
                sch.dma("sp", "hst%d" % slot, lambda e, o=hsave[g], i=hT.ap[:]: e.dma_start(out=o, in_=i),
                        reads=hT.b, writes=[self.dregion("hsave", g)])
            for j in range(nj):
                pg, pu = self.bank[(j % 2) * 2], self.bank[(j % 2) * 2 + 1]
                wg, wu, jo = wgs[j // 2], wus[j // 2], (j % 2) * 128
                for k in range(KC):
                    sch.op("pe", lambda e, k=k, jo=jo, pg=pg, hT=hT, wg=wg: e.matmul(
                        pg.ap[:], lhsT=wg.v[:, k, jo:jo + 128], rhs=hT.ap[:, k, :],
                        start=(k == 0), stop=(k == KC - 1)), reads=hT.b + wg.b(k), writes=pg.b)
                for k in range(KC):
                    sch.op("pe", lambda e, k=k, jo=jo, pu=pu, hT=hT, wu=wu: e.matmul(
                        pu.ap[:], lhsT=wu.v[:, k, jo:jo + 128], rhs=hT.ap[:, k, :],
                        start=(k == 0), stop=(k == KC - 1)), reads=hT.b + wu.b(k), writes=pu.b)
                sg = sgs[j % 2]
                sch.op("act", lambda e, pg=pg, sg=sg: e.activation(out=sg.ap[:], in_=pg.ap[:], func=AF.Silu),
                       reads=pg.b, writes=sg.b)
                sch.op("dve", lambda e, j=j, pu=pu, sg=sg, hid=hid: e.tensor_tensor(
                    out=hid.ap[:, j, :], in0=pu.ap[:], in1=sg.ap[:], op=ALU.mult),
                    reads=pu.b + sg.b, writes=hid.b)
            self.out_proj(hid, nj, wo, 0.5, slot)
            tok = self.store_x(dst, dname, g, slot)
        self.gi += self.NG
        return tok

    def ffn(self, w_in, w_out, l, src, sname, tmpA, tmpB, dst, dname, gain_col):
        self.ffn_pass(w_in, w_out, l, 0, 8, src, sname, tmpA[0], tmpA[1], gain_col, True)
        self.ffn_pass(w_in, w_out, l, 8, 7, tmpA[0], tmpA[1], tmpB[0], tmpB[1], gain_col, False)
        return self.ffn_pass(w_in, w_out, l, 15, 7, tmpB[0], tmpB[1], dst, dname, gain_col, False)

    def memkv(self, mem, mem_w_kv):
        sch = self.sch
        self.phase_begin()
        w = self.wload(mem_w_kv.rearrange("(kc p) n -> p kc n", p=128), KC, 2 * C)
        hT = self.r16([KC, G])
        xn = self.r16([NT, C])
        sq = self.r16([C])
        slot = self.gi % 2
        self.gi += 1
        view = mem[:, :].rearrange("(t p) c -> p t c", p=128)
        sch.dma("sp", "xld%d" % slot, lambda e, o=self.xt[slot].ap[:, 0:2, :], i=view: e.dma_start(out=o, in_=i),
                writes=self.xt[slot].b)
        self.norm_T(slot, hT, xn, sq, COLS["mem_norm"], nt=2)
        for m in range(KC):
            bk = self.bank[m % 2]
            for k in range(KC):
                sch.op("pe", lambda e, k=k, m=m, bk=bk: e.matmul(
                    bk.ap[:, 0:NMEM], lhsT=w.v[:, k, m * 128:(m + 1) * 128], rhs=hT.ap[:, k, 0:NMEM],
                    start=(k == 0), stop=(k == KC - 1)), reads=hT.b + w.b(k), writes=bk.b)
            sch.op("act", lambda e, m=m, bk=bk: e.activation(out=self.KT.ap[:, m, :], in_=bk.ap[:, 0:NMEM],
                                                           func=AF.Copy), reads=bk.b, writes=self.KT.b)
        for mc in range(2):
            for nb in range(2):
                bk = self.bank[2 + (mc * 2 + nb) % 2]
                for k in range(KC):
                    sch.op("pe", lambda e, k=k, mc=mc, nb=nb, bk=bk: e.matmul(
                        bk.ap[:], lhsT=hT.ap[:, k, mc * 128:(mc + 1) * 128],
                        rhs=w.v[:, k, C + nb * 512:C + (nb + 1) * 512],
                        start=(k == 0), stop=(k == KC - 1)), reads=hT.b + w.b(k), writes=bk.b)
                sch.op("dve", lambda e, mc=mc, nb=nb, bk=bk: e.tensor_copy(
                    out=self.Vm.ap[:, mc, nb * 512:(nb + 1) * 512], in_=bk.ap[:]), reads=bk.b, writes=self.Vm.b)

    def xattn(self, wq_d, wo_d, l, src, sname, dst, dname, gain_col):
        sch = self.sch
        self.phase_begin()
        wq = self.wload(wq_d[l].rearrange("(kc p) n -> p kc n", p=128), KC, C)
        wo = self.wload(wo_d[l].rearrange("(kc p) n -> p kc n", p=128), KC, C)
        hT = self.r16([KC, G])
        xn = self.r16([NT, C])
        sq = self.r16([C])
        qT = self.r16([KC, G])
        oT = self.r16([KC, G])
        exs = [self.r16([2, G]) for _ in range(2)]
        rden = self.a32([G])
        KT, Vm = self.KT, self.Vm
        gi0 = self.gi
        self.load_x(src, sname, 0, gi0 % 2)
        tok = None
        for g in range(self.NG):
            slot = (gi0 + g) % 2
            if g + 1 < self.NG:
                self.load_x(src, sname, g + 1, (gi0 + g + 1) % 2)
            self.norm_T(slot, hT, xn, sq, gain_col)
            for m in range(KC):
                bk = self.bank[m % 2]
                for k in range(KC):
                    sch.op("pe", lambda e, k=k, m=m, bk=bk: e.matmul(
                        bk.ap[:], lhsT=wq.v[:, k, m * 128:(m + 1) * 128], rhs=hT.ap[:, k, :],
                        start=(k == 0), stop=(k == KC - 1)), reads=hT.b + wq.b(k), writes=bk.b)
                sch.op("act", lambda e, m=m, bk=bk: e.activation(out=qT.ap[:, m, :], in_=bk.ap[:], func=AF.Copy,
                                                               scale=1.0 / 16.0), reads=bk.b, writes=qT.b)
            for hh in range(4):
                ex = exs[hh % 2]
                for mc in range(2):
                    bk = self.bank[2 + mc]
                    for j in range(2):
                        sch.op("pe", lambda e, j=j, mc=mc, hh=hh, bk=bk: e.matmul(
                            bk.ap[:], lhsT=KT.ap[:, 2 * hh + j, mc * 128:(mc + 1) * 128], rhs=qT.ap[:, 2 * hh + j, :],
                            start=(j == 0), stop=(j == 1)), reads=KT.b + qT.b, writes=bk.b)
                    sch.op("act", lambda e, mc=mc, bk=bk, ex=ex: e.activation(out=ex.ap[:, mc, :], in_=bk.ap[:],
                                                                             func=AF.Exp), reads=bk.b, writes=ex.b)
                bd = self.bank[4]
                for mc in range(2):
                    sch.op("pe", lambda e, mc=mc, ex=ex, bd=bd: e.matmul(
                        bd.ap[:], lhsT=self.ones.ap[:], rhs=ex.ap[:, mc, :], start=(mc == 0), stop=(mc == 1)),
                        reads=self.ones.b + ex.b, writes=bd.b)
                sch.op("dve", lambda e, bd=bd: e.reciprocal(out=rden.ap[:], in_=bd.ap[:]), reads=bd.b, writes=rden.b)
                for j in range(2):
                    bo = self.bank[j]
                    for mc in range(2):
                        sch.op("pe", lambda e, mc=mc, j=j, hh=hh, ex=ex, bo=bo: e.matmul(
                            bo.ap[:], lhsT=Vm.ap[:, mc, (2 * hh + j) * 128:(2 * hh + j + 1) * 128], rhs=ex.ap[:, mc, :],
                            start=(mc == 0), stop=(mc == 1)), reads=Vm.b + ex.b, writes=bo.b)
                    sch.op("dve", lambda e, j=j, hh=hh, bo=bo: e.tensor_tensor(
                        out=oT.ap[:, 2 * hh + j, :], in0=bo.ap[:], in1=rden.ap[:], op=ALU.mult),
                        reads=bo.b + rden.b, writes=oT.b)
            self.out_proj(oT, KC, wo, 1.0, slot)
            tok = self.store_x(dst, dname, g, slot)
        self.gi += self.NG
        return tok

    def gmlp(self, w_uv, wsT_d, bsT_d, w_o, src, sname, dst, dname, gain_col, final=None):
        sch = self.sch
        self.phase_begin(reset=True)
        wuv = w_uv[0].rearrange("(kc p) n -> p kc n", p=128)
        wu = self.wload(wuv[:, :, 0:2048], KC, 2048)
        wv = self.wload(wuv[:, :, 2048:4096], KC, 2048)
        wo = self.wload(w_o[0].rearrange("(kc p) n -> p kc n", p=128), 16, C)
        ws = self.wload(wsT_d.rearrange("s (g t) -> s g t", g=16), 16, 128)
        sch.op("pool", lambda e: e.affine_select(out=ws.v[:], in_=ws.v[:], pattern=[[0, 16], [1, 128]],
                                                 compare_op=ALU.is_ge, fill=0.0, base=0, channel_multiplier=-1),
               reads=ws.all(), writes=ws.all())
        bsT = self.a32([16, 128])
        sch.dma("sp", "bsT", lambda e: e.dma_start(out=bsT.ap[:], in_=bsT_d.rearrange("p (g t) -> p g t", g=16)),
                writes=bsT.b)
        hT = self.r16([KC, G])
        vt = self.r16([NT, 2048])
        xn = T(vt.ap[:, 0:2, :].rearrange("p a (b c) -> p (a b) c", c=C), vt.b[0:4])
        uT = self.r16([16, G])
        sq = T(uT.ap[:, 0:4, :].rearrange("p a b -> p (a b)"), uT.b[0:4])
        svs = [self.a32([G]) for _ in range(2)]
        gvc = COLS["gv_norm"]
        ss, rs = self.ss, self.rs
        gi0 = self.gi
        self.load_x(src, sname, 0, gi0 % 2)
        tok = None
        for g in range(self.NG):
            slot = (gi0 + g) % 2
            if g + 1 < self.NG:
                self.load_x(src, sname, g + 1, (gi0 + g + 1) % 2)
            self.norm_T(slot, hT, xn, T(sq.ap[:, 0:C], sq.b), gain_col)
            for t in range(NT):
                for nb in range(4):
                    bk = self.bank[2 + nb % 2]
                    for k in range(KC):
                        sch.op("pe", lambda e, k=k, t=t, nb=nb, bk=bk: e.matmul(
                            bk.ap[:], lhsT=hT.ap[:, k, t * 128:(t + 1) * 128], rhs=wv.v[:, k, nb * 512:(nb + 1) * 512],
                            start=(k == 0), stop=(k == KC - 1)), reads=hT.b + wv.b(k), writes=bk.b)
                    sch.op("act", lambda e, t=t, nb=nb, bk=bk: e.activation(
                        out=vt.ap[:, t, nb * 512:(nb + 1) * 512], in_=bk.ap[:], func=AF.Gelu), reads=bk.b, writes=vt.b)
                sch.op("act", lambda e, t=t: e.activation(out=sq.ap[:], in_=vt.ap[:, t, :], func=AF.Square,
                                                          accum_out=ss.ap[:, t:t + 1]), reads=vt.b, writes=sq.b + ss.b)
            sch.op("act", lambda e: e.activation(out=rs.ap[:], in_=ss.ap[:], func=AF.Sqrt,
                                                 bias=float(RMS_EPS), scale=1.0 / 2048.0), reads=ss.b, writes=rs.b)
            sch.op("dve", lambda e: e.reciprocal(out=rs.ap[:], in_=rs.ap[:]), reads=rs.b, writes=rs.b)
            for t in range(NT):
                sch.op("dve", lambda e, t=t: e.tensor_scalar(out=vt.ap[:, t, :], in0=vt.ap[:, t, :],
                                                             scalar1=rs.ap[:, t:t + 1], scalar2=None, op0=ALU.mult),
                       reads=vt.b + rs.b, writes=vt.b)
            for gc in range(16):
                bk = self.bank[gc % 2]
                for k in range(KC):
                    sch.op("pe", lambda e, k=k, gc=gc, bk=bk: e.matmul(
                        bk.ap[:], lhsT=wu.v[:, k, gc * 128:(gc + 1) * 128], rhs=hT.ap[:, k, :],
                        start=(k == 0), stop=(k == KC - 1)), reads=hT.b + wu.b(k), writes=bk.b)
                sch.op("act", lambda e, gc=gc, bk=bk: e.activation(out=uT.ap[:, gc, :], in_=bk.ap[:], func=AF.Gelu),
                       reads=bk.b, writes=uT.b)
                bs_ = self.bank[3 + gc % 2]
                for t in range(NT):
                    sch.op("pe", lambda e, t=t, gc=gc, bs_=bs_: e.matmul(
                        bs_.ap[:, t * 128:(t + 1) * 128], lhsT=vt.ap[:, t, gc * 128:(gc + 1) * 128], rhs=ws.v[:, gc, :],
                        start=True, stop=True), reads=vt.b + ws.all(), writes=bs_.b)
                sv = svs[gc % 2]
                sch.op("dve", lambda e, gc=gc, bs_=bs_, sv=sv: e.scalar_tensor_tensor(
                    out=sv.ap[:].rearrange("p (t s) -> p t s", t=NT), in0=bs_.ap[:].rearrange("p (t s) -> p t s", t=NT),
                    scalar=self.cols[:, gvc + gc:gvc + gc + 1],
                    in1=bsT.ap[:, gc, :].unsqueeze(1).broadcast_to([128, NT, 128]),
                    op0=ALU.mult, op1=ALU.add), reads=bs_.b + [self.bcols] + bsT.b, writes=sv.b)
                sch.op("dve", lambda e, gc=gc, sv=sv: e.tensor_tensor(out=uT.ap[:, gc, :], in0=uT.ap[:, gc, :],
                                                                     in1=sv.ap[:], op=ALU.mult),
                       reads=uT.b + sv.b, writes=uT.b)
            self.out_proj(uT, 16, wo, 1.0, slot)
            tok = self.store_x(dst, dname, g, slot)
        self.gi += self.NG
        return tok

    def final_norm(self, gain_bc_d, src, sname, dst, dname):
        sch = self.sch
        self.phase_begin()
        gbc = self.a32([C])
        sch.dma("sp", "gbc", lambda e: e.dma_start(out=gbc.ap[:], in_=gain_bc_d[:, :]), writes=gbc.b)
        sq = self.r16([C])
        ss, rs = self.ss, self.rs
        gi0 = self.gi
        self.load_x(src, sname, 0, gi0 % 2)
        toks = []
        for g in range(self.NG):
            slot = (gi0 + g) % 2
            xt = self.xt[slot]
            if g + 1 < self.NG:
                self.load_x(src, sname, g + 1, (gi0 + g + 1) % 2)
            for t in range(NT):
                sch.op("act", lambda e, t=t, xt=xt: e.activation(out=sq.ap[:], in_=xt.ap[:, t, :], func=AF.Square,
                                                                 accum_out=ss.ap[:, t:t + 1]),
                       reads=xt.b, writes=sq.b + ss.b)
            sch.op("act", lambda e: e.activation(out=rs.ap[:], in_=ss.ap[:], func=AF.Sqrt,
                                                 bias=float(RMS_EPS), scale=1.0 / C), reads=ss.b, writes=rs.b)
            sch.op("dve", lambda e: e.reciprocal(out=rs.ap[:], in_=rs.ap[:]), reads=rs.b, writes=rs.b)
            for t in range(NT):
                sch.op("dve", lambda e, t=t, xt=xt: e.scalar_tensor_tensor(
                    out=xt.ap[:, t, :], in0=xt.ap[:, t, :], scalar=rs.ap[:, t:t + 1], in1=gbc.ap[:],
                    op0=ALU.mult, op1=ALU.mult), reads=xt.b + rs.b + gbc.b, writes=xt.b)
            toks.append(self.store_x(dst, dname, g, slot))
        self.gi += self.NG
        return toks

    def rwkv_consts(self):
        sch = self.sch
        self.oblk = self.fixed("oblk", [128, 128], BF16)
        self.hind = self.fixed("hind", [128, KC, 16], BF16)
        self.rmask = self.fixed("rmask", [128, G], BF16)
        self.st = self.fixed("st", [128, 4, 16], F32)
        self.mAT = self.fixed("mAT", [128, 2, 128], F32)
        self.mMN = self.fixed("mMN", [64, 2, 64], F32)
        self.bdm = self.fixed("bdm", [128, 128], F32)
        P = "pool"
        sch.op(P, lambda e: e.memset(self.oblk.ap[:], 0.0), writes=self.oblk.b)
        sch.op(P, lambda e: e.memset(self.oblk.ap[0:64, 0:64], 1.0), writes=self.oblk.b)
        sch.op(P, lambda e: e.memset(self.oblk.ap[64:128, 64:128], 1.0), writes=self.oblk.b)
        sch.op(P, lambda e: e.memset(self.bdm.ap[:], 0.0), writes=self.bdm.b)
        sch.op(P, lambda e: e.memset(self.bdm.ap[0:64, 0:64], 1.0), writes=self.bdm.b)
        sch.op(P, lambda e: e.memset(self.bdm.ap[64:128, 64:128], 1.0), writes=self.bdm.b)
        sch.op(P, lambda e: e.memset(self.hind.ap[:], 0.0), writes=self.hind.b)
        for c in range(KC):
            sch.op(P, lambda e, c=c: e.memset(self.hind.ap[0:64, c, 2 * c:2 * c + 1], 1.0), writes=self.hind.b)
            sch.op(P, lambda e, c=c: e.memset(self.hind.ap[64:128, c, 2 * c + 1:2 * c + 2], 1.0), writes=self.hind.b)
        sch.op(P, lambda e: e.memset(self.rmask.ap[:], 1.0), writes=self.rmask.b)
        sch.op(P, lambda e: e.memset(self.rmask.ap[:].rearrange("p (c s) -> p c s", s=LCH)[:, :, 0:1], 0.0),
               writes=self.rmask.b)
        sch.op(P, lambda e: e.memset(self.mMN.ap[:], 1.0), writes=self.mMN.b)
        sch.op(P, lambda e: e.affine_select(out=self.mMN.ap[:, 0, :], in_=self.mMN.ap[:, 0, :], pattern=[[1, 64]],
                                            compare_op=ALU.is_ge, fill=0.0, base=-1, channel_multiplier=-1),
               reads=self.mMN.b, writes=self.mMN.b)
        sch.op(P, lambda e: e.affine_select(out=self.mMN.ap[:, 1, :], in_=self.mMN.ap[:, 1, :], pattern=[[-1, 64]],
                                            compare_op=ALU.is_ge, fill=0.0, base=-1, channel_multiplier=1),
               reads=self.mMN.b, writes=self.mMN.b)
        sch.op(P, lambda e: e.memset(self.mAT.ap[:], 1.0), writes=self.mAT.b)
        for q in range(2):
            for blk in range(2):
                rows = slice(blk * 64, blk * 64 + 64)
                if blk == q:
                    sch.op(P, lambda e, q=q, rows=rows: e.affine_select(
                        out=self.mAT.ap[rows, q, 0:64], in_=self.mAT.ap[rows, q, 0:64], pattern=[[1, 64]],
                        compare_op=ALU.is_ge, fill=0.0, base=-1, channel_multiplier=-1),
                        reads=self.mAT.b, writes=self.mAT.b)
                else:
                    sch.op(P, lambda e, q=q, rows=rows: e.memset(self.mAT.ap[rows, q, 0:64], 0.0), writes=self.mAT.b)
                sch.op(P, lambda e, q=q, rows=rows: e.affine_select(
                    out=self.mAT.ap[rows, q, 64:128], in_=self.mAT.ap[rows, q, 64:128], pattern=[[1, 64]],
                    compare_op=ALU.is_ge, fill=0.0, base=0, channel_multiplier=-1),
                    reads=self.mAT.b, writes=self.mAT.b)

    def rwkv_r1(self, W, src, sname, D):
        sch = self.sch
        self.phase_begin(reset=True)
        wrkv = W["w_rkv"][0].rearrange("(kc p) n -> p kc n", p=128)
        wr = self.wload(wrkv[:, :, 0:C], KC, C)
        wk = self.wload(wrkv[:, :, C:2 * C], KC, C)
        wv = self.wload(wrkv[:, :, 2 * C:3 * C], KC, C)
        w1 = self.wload(W["w1"][0].rearrange("(kc p) n -> p kc n", p=128), KC, 64)
        a1 = self.wload(W["a1"][0].rearrange("(kc p) n -> p kc n", p=128), KC, 64)
        g1 = self.wload(W["g1"][0].rearrange("(kc p) n -> p kc n", p=128), KC, 160)
        w2 = self.wload(W["w2"][0].rearrange("(o p) n -> p o n", o=1), 1, C, part=64)
        a2 = self.wload(W["a2"][0].rearrange("(o p) n -> p o n", o=1), 1, C, part=64)
        g2a = self.wload(W["g2"][0][0:128, :].rearrange("(o p) n -> p o n", o=1), 1, C)
        g2b = self.wload(W["g2"][0][128:160, :].rearrange("(o p) n -> p o n", o=1), 1, C, part=32)
        hx = self.r16([KC, G + 4])
        xn = self.r16([NT, C])
        dx = T(xn.ap[:].rearrange("p t (a b) -> p (t a) b", b=G), xn.b)
        Ls = [self.r16([KC, G]) for _ in range(2)]
        t1 = self.r16([G], part=64)
        a1o = self.r16([G], part=64)
        sgT = self.r16([2, G])
        vtm = self.r16([NT, C])
        gtm = self.r16([NT, C])
        rkT = self.r16([KC, G])
        sq = T(rkT.ap[:, 0:2, :].rearrange("p a b -> p (a b)"), rkT.b[0:1])
        ARs = [self.r16([NCH, 128]) for _ in range(2)]
        BKs = [self.r16([NCH, 128]) for _ in range(2)]
        ksq = self.r16([G])
        SG, CUM, GE, IGE, GX, A_, KK, KP, X1 = [self.a32([G]) for _ in range(9)]
        GLs = self.a32([KC, NCH])
        bon = self.a32([NT, 16])
        mu0 = COLS["mu"]
        cols = self.cols
        gi0 = self.gi
        sch.op("pool", lambda e: e.memset(hx.ap[:, :, 0:1], 0.0), writes=hx.b)
        self.load_x(src, sname, 0, gi0 % 2)

        def lerp(j, L):
            for c in range(KC):
                sch.op("dve", lambda e, c=c, j=j, L=L: e.scalar_tensor_tensor(
                    out=L.ap[:, c, :], in0=dx.ap[:, c, :], scalar=cols[:, mu0 + j * 8 + c:mu0 + j * 8 + c + 1],
                    in1=hx.ap[:, c, 1:G + 1], op0=ALU.mult, op1=ALU.add),
                    reads=dx.b + hx.b + [self.bcols], writes=L.b)

        for g in range(self.NG):
            slot = (gi0 + g) % 2
            if g + 1 < self.NG:
                self.load_x(src, sname, g + 1, (gi0 + g + 1) % 2)
            if g > 0:
                sch.op("pool", lambda e: e.tensor_copy(out=hx.ap[:, :, 0:1], in_=hx.ap[:, :, G:G + 1]),
                       reads=hx.b, writes=hx.b)
            self.norm_T(slot, hx, xn, sq, COLS["mix_norm0"], col0=1)
            sch.op("dve", lambda e: e.tensor_tensor(out=dx.ap[:], in0=hx.ap[:, :, 0:G], in1=hx.ap[:, :, 1:G + 1],
                                                    op=ALU.subtract), reads=hx.b, writes=dx.b)
            lerp(1, Ls[0])
            bk = self.bank[0]
            for k in range(KC):
                sch.op("pe", lambda e, k=k, bk=bk: e.matmul(bk.ap[0:64, :], lhsT=w1.v[:, k, :], rhs=Ls[0].ap[:, k, :],
                                                            start=(k == 0), stop=(k == KC - 1)),
                       reads=Ls[0].b + w1.all(), writes=bk.b)
            sch.op("act", lambda e, bk=bk: e.activation(out=t1.ap[:], in_=bk.ap[0:64, :], func=AF.Tanh),
                   reads=bk.b, writes=t1.b)
            lerp(4, Ls[1])
            bk = self.bank[1]
            for k in range(KC):
                sch.op("pe", lambda e, k=k, bk=bk: e.matmul(bk.ap[0:64, :], lhsT=a1.v[:, k, :], rhs=Ls[1].ap[:, k, :],
                                                            start=(k == 0), stop=(k == KC - 1)),
                       reads=Ls[1].b + a1.all(), writes=bk.b)
            sch.op("act", lambda e, bk=bk: e.activation(out=a1o.ap[:], in_=bk.ap[0:64, :], func=AF.Copy),
                   reads=bk.b, writes=a1o.b)
            lerp(5, Ls[0])
            for m, (m0, mw) in enumerate([(0, 128), (128, 32)]):
                bk = self.bank[2 + m]
                for k in range(KC):
                    sch.op("pe", lambda e, k=k, bk=bk, m0=m0, mw=mw: e.matmul(
                        bk.ap[0:mw, :], lhsT=g1.v[:, k, m0:m0 + mw], rhs=Ls[0].ap[:, k, :],
                        start=(k == 0), stop=(k == KC - 1)), reads=Ls[0].b + g1.all(), writes=bk.b)
                sch.op("act", lambda e, bk=bk, m=m, mw=mw: e.activation(out=sgT.ap[0:mw, m, :], in_=bk.ap[0:mw, :],
                                                                       func=AF.Sigmoid), reads=bk.b, writes=sgT.b)
            for t in range(NT):
                for nb in range(2):
                    bk = self.bank[(t * 2 + nb) % 2]
                    sch.op("pe", lambda e, t=t, nb=nb, bk=bk: e.matmul(
                        bk.ap[:], lhsT=sgT.ap[:, 0, t * 128:(t + 1) * 128], rhs=g2a.v[:, 0, nb * 512:(nb + 1) * 512],
                        start=True, stop=False), reads=sgT.b + g2a.all(), writes=bk.b)
                    sch.op("pe", lambda e, t=t, nb=nb, bk=bk: e.matmul(
                        bk.ap[:], lhsT=sgT.ap[0:32, 1, t * 128:(t + 1) * 128], rhs=g2b.v[:, 0, nb * 512:(nb + 1) * 512],
                        start=False, stop=True), reads=sgT.b + g2b.all(), writes=bk.b)
                    sch.op("act", lambda e, t=t, nb=nb, bk=bk: e.activation(
                        out=gtm.ap[:, t, nb * 512:(nb + 1) * 512], in_=bk.ap[:], func=AF.Copy),
                        reads=bk.b, writes=gtm.b)
            sch.dma("sp", "gst", lambda e, g=g: e.dma_start(
                out=D["Gt"][g * G:(g + 1) * G, :].rearrange("(t p) c -> p t c", p=128), in_=gtm.ap[:]),
                reads=gtm.b, writes=[self.dregion("Gt", g)])
            lerp(3, Ls[1])
            for t in range(NT):
                for nb in range(2):
                    bk = self.bank[2 + (t * 2 + nb) % 2]
                    for k in range(KC):
                        sch.op("pe", lambda e, k=k, t=t, nb=nb, bk=bk: e.matmul(
                            bk.ap[:], lhsT=Ls[1].ap[:, k, t * 128:(t + 1) * 128], rhs=wv.v[:, k, nb * 512:(nb + 1) * 512],
                            start=(k == 0), stop=(k == KC - 1)), reads=Ls[1].b + wv.b(k), writes=bk.b)
                    sch.op("act", lambda e, t=t, nb=nb, bk=bk: e.activation(
                        out=vtm.ap[:, t, nb * 512:(nb + 1) * 512], in_=bk.ap[:], func=AF.Copy),
                        reads=bk.b, writes=vtm.b)
            sch.dma("sp", "vst", lambda e, g=g: e.dma_start(
                out=D["V"][g * G:(g + 1) * G, :].rearrange("(t p) c -> p t c", p=128), in_=vtm.ap[:]),
                reads=vtm.b, writes=[self.dregion("V", g)])
            lerp(0, Ls[0])
            lerp(2, Ls[1])
            for m in range(KC):
                br, bkk, bz, ba, bs = self.bank[m % 2], self.bank[2 + m % 2], self.bank[4], self.bank[5], self.bank[4]
                for k in range(KC):
                    sch.op("pe", lambda e, k=k, m=m, br=br: e.matmul(br.ap[:], lhsT=wr.v[:, k, m * 128:(m + 1) * 128],
                                                             rhs=Ls[0].ap[:, k, :], start=(k == 0), stop=(k == KC - 1)),
                           reads=Ls[0].b + wr.b(k), writes=br.b)
                for k in range(KC):
                    sch.op("pe", lambda e, k=k, m=m, bkk=bkk: e.matmul(bkk.ap[:], lhsT=wk.v[:, k, m * 128:(m + 1) * 128],
                                                             rhs=Ls[1].ap[:, k, :], start=(k == 0), stop=(k == KC - 1)),
                           reads=Ls[1].b + wk.b(k), writes=bkk.b)
                sch.op("pe", lambda e, m=m: e.matmul(bz.ap[:], lhsT=w2.v[:, 0, m * 128:(m + 1) * 128], rhs=t1.ap[:],
                                                     start=True, stop=True), reads=t1.b + w2.all(), writes=bz.b)
                sch.op("pe", lambda e, m=m: e.matmul(ba.ap[:], lhsT=a2.v[:, 0, m * 128:(m + 1) * 128], rhs=a1o.ap[:],
                                                     start=True, stop=True), reads=a1o.b + a2.all(), writes=ba.b)
                cw0, ca0 = COLS["w0"] + m, COLS["a0"] + m
                ckk, cka, com, crk = COLS["k_k"] + m, COLS["k_a"] + m, COLS["omka"] + m, COLS["r_k"] + m
                bc = [self.bcols]
                sch.op("act", lambda e, cw0=cw0: e.activation(out=SG.ap[:], in_=bz.ap[:], func=AF.Sigmoid,
                                                              bias=cols[:, cw0:cw0 + 1]), reads=bz.b + bc, writes=SG.b)
                sch.op("act", lambda e, ca0=ca0: e.activation(out=A_.ap[:], in_=ba.ap[:], func=AF.Sigmoid,
                                                              bias=cols[:, ca0:ca0 + 1]), reads=ba.b + bc, writes=A_.b)
                sch.op("dve", lambda e: e.tensor_tensor_scan(out=CUM.ap[:], data0=self.rmask.ap[:], data1=SG.ap[:],
                                                             initial=0.0, op0=ALU.mult, op1=ALU.add),
                       reads=self.rmask.b + SG.b, writes=CUM.b)
                sch.op("act", lambda e: e.activation(out=GE.ap[:], in_=CUM.ap[:], func=AF.Exp, scale=-DEC),
                       reads=CUM.b, writes=GE.b)
                sch.op("act", lambda e: e.activation(out=IGE.ap[:], in_=CUM.ap[:], func=AF.Exp, scale=DEC),
                       reads=CUM.b, writes=IGE.b)
                sch.op("dve", lambda e: e.tensor_tensor(out=X1.ap[:], in0=CUM.ap[:], in1=SG.ap[:], op=ALU.subtract),
                       reads=CUM.b + SG.b, writes=X1.b)
                sch.op("act", lambda e: e.activation(out=GX.ap[:], in_=X1.ap[:], func=AF.Exp, scale=-DEC),
                       reads=X1.b, writes=GX.b)
                sch.op("dve", lambda e, ckk=ckk, bkk=bkk: e.tensor_scalar(out=KK.ap[:], in0=bkk.ap[:], scalar1=cols[:, ckk:ckk + 1],
                                                                 scalar2=None, op0=ALU.mult), reads=bkk.b + bc, writes=KK.b)
                sch.op("act", lambda e: e.activation(out=ksq.ap[:], in_=KK.ap[:], func=AF.Square),
                       reads=KK.b, writes=ksq.b)
                sch.op("pe", lambda e: e.matmul(bs.ap[:], lhsT=self.oblk.ap[:], rhs=ksq.ap[:], start=True, stop=True),
                       reads=self.oblk.b + ksq.b, writes=bs.b)
                sch.op("act", lambda e: e.activation(out=X1.ap[:], in_=bs.ap[:], func=AF.Sqrt, bias=1e-24),
                       reads=bs.b, writes=X1.b)
                sch.op("dve", lambda e: e.reciprocal(out=X1.ap[:], in_=X1.ap[:]), reads=X1.b, writes=X1.b)
                sch.op("dve", lambda e: e.tensor_tensor(out=KK.ap[:], in0=KK.ap[:], in1=X1.ap[:], op=ALU.mult),
                       reads=KK.b + X1.b, writes=KK.b)
                AR, BK = ARs[m % 2], BKs[m % 2]
                c3 = lambda t_: t_.ap[:].rearrange("p (c s) -> p c s", s=LCH)
                sch.op("dve", lambda e, AR=AR: e.scalar_tensor_tensor(out=AR.ap[:, :, 0:64], in0=c3(KK), scalar=-1.0,
                                                                      in1=c3(GX), op0=ALU.mult, op1=ALU.mult),
                       reads=KK.b + GX.b, writes=AR.b)
                sch.op("dve", lambda e, AR=AR, br=br: e.tensor_tensor(out=AR.ap[:, :, 64:128],
                                                               in0=br.ap[:].rearrange("p (c s) -> p c s", s=LCH),
                                                               in1=c3(GE), op=ALU.mult), reads=br.b + GE.b, writes=AR.b)
                sch.op("dve", lambda e: e.tensor_tensor(out=X1.ap[:], in0=KK.ap[:], in1=A_.ap[:], op=ALU.mult),
                       reads=KK.b + A_.b, writes=X1.b)
                for q in range(2):
                    sch.op("dve", lambda e, q=q, BK=BK: e.tensor_tensor(
                        out=BK.ap[:, q::2, (1 - q) * 64:(1 - q) * 64 + 64], in0=c3(X1)[:, q::2, :], in1=c3(IGE)[:, q::2, :],
                        op=ALU.mult), reads=X1.b + IGE.b, writes=BK.b)
                sch.op("dve", lambda e, cka=cka, com=com: e.tensor_scalar(
                    out=KP.ap[:], in0=A_.ap[:], scalar1=cols[:, cka:cka + 1], scalar2=cols[:, com:com + 1],
                    op0=ALU.mult, op1=ALU.add), reads=A_.b + bc, writes=KP.b)
                sch.op("dve", lambda e, bkk=bkk: e.tensor_tensor(out=KP.ap[:], in0=bkk.ap[:], in1=KP.ap[:], op=ALU.mult),
                       reads=bkk.b + KP.b, writes=KP.b)
                for q in range(2):
                    sch.op("pool" if q == 0 else "dve", lambda e, q=q, BK=BK: e.tensor_tensor(
                        out=BK.ap[:, q::2, q * 64:q * 64 + 64], in0=c3(KP)[:, q::2, :], in1=c3(IGE)[:, q::2, :],
                        op=ALU.mult), reads=KP.b + IGE.b, writes=BK.b)
                sch.op("dve", lambda e, br=br: e.tensor_tensor(out=X1.ap[:], in0=br.ap[:], in1=KP.ap[:], op=ALU.mult),
                       reads=br.b + KP.b, writes=X1.b)
                sch.op("act", lambda e, m=m, crk=crk: e.activation(out=rkT.ap[:, m, :], in_=X1.ap[:], func=AF.Copy,
                                                                   scale=cols[:, crk:crk + 1]),
                       reads=X1.b + bc, writes=rkT.b)
                sch.op("act", lambda e, m=m: e.activation(out=GLs.ap[:, m, :], in_=GE.ap[:, LCH - 1::LCH], func=AF.Copy),
                       reads=GE.b, writes=GLs.b)
                sch.dma("sp", "arst%d" % (m % 2), lambda e, g=g, m=m, AR=AR: e.dma_start(out=D["AR"][g, :, m], in_=AR.ap[:]),
                        reads=AR.b, writes=[self.dregion("AR", g)])
                sch.dma("sp", "bkst%d" % (m % 2), lambda e, g=g, m=m, BK=BK: e.dma_start(out=D["BK"][g, :, m], in_=BK.ap[:]),
                        reads=BK.b, writes=[self.dregion("BK", g)])
            sch.dma("sp", "glst", lambda e, g=g: e.dma_start(out=D["GL"][g], in_=GLs.ap[:]),
                    reads=GLs.b, writes=[self.dregion("GL", g)])
            bb = self.bank[5]
            for t in range(NT):
                for m in range(KC):
                    sch.op("pe", lambda e, t=t, m=m: e.matmul(bb.ap[:, t * 16:(t + 1) * 16],
                                                             lhsT=rkT.ap[:, m, t * 128:(t + 1) * 128],
                                                             rhs=self.hind.ap[:, m, :], start=(m == 0), stop=(m == KC - 1)),
                           reads=rkT.b + self.hind.b, writes=bb.b)
            sch.op("act", lambda e: e.activation(out=bon.ap[:].rearrange("p t h -> p (t h)"), in_=bb.ap[:, 0:NT * 16],
                                                 func=AF.Copy), reads=bb.b, writes=bon.b)
            sch.dma("sp", "bonst", lambda e, g=g: e.dma_start(
                out=D["BON"][g * G:(g + 1) * G, :].rearrange("(t p) h -> p t h", p=128), in_=bon.ap[:]),
                reads=bon.b, writes=[self.dregion("BON", g)])
        self.gi += self.NG

    def rwkv_r2(self, w_o_d, lnw_d, lnb_d, D, base, bname, dst, dname):
        sch = self.sch
        self.phase_begin(reset=True)
        wo = self.wload(w_o_d[0].rearrange("(kc p) n -> p kc n", p=128), KC, C)
        ARg = [self.r16([KC, NCH, 128]) for _ in range(2)]
        BKg = [self.r16([KC, NCH, 128]) for _ in range(2)]
        UV = [self.r16([16, 64]) for _ in range(2)]
        ATs = [self.r16([16, 128]) for _ in range(2)]
        Ms = [self.r16([16, 64], part=64) for _ in range(2)]
        Ns = [self.r16([16, 64], part=64) for _ in range(2)]
        PTs = [self.r16([16, 64], part=64) for _ in range(2)]
        Wsb = self.r16([16, 64], part=64)
        BKtm = self.r16([KC, 128])
        Hb = self.r16([KC, 128])
        yg = self.r16([C])
        ygT = self.r16([KC, G])
        gts = [self.r16([C]) for _ in range(2)]
        ysq = self.r16([16, 64])
        Hf = self.a32([KC, 128])
        ysb = self.a32([16, 64])
        lnw = self.a32([16, 64])
        lnb = self.a32([16, 64])
        GLg = [self.a32([KC, NCH]) for _ in range(2)]
        bons = [self.a32([16]) for _ in range(2)]
        st = self.st
        idf64 = self.ident_f[0:64, 0:64]
        sch.dma("sp", "lnw", lambda e: e.dma_start(out=lnw.ap[:].rearrange("p h i -> p (h i)"), in_=lnw_d[:, :]), writes=lnw.b)
        sch.dma("sp", "lnb", lambda e: e.dma_start(out=lnb.ap[:].rearrange("p h i -> p (h i)"), in_=lnb_d[:, :]), writes=lnb.b)
        sch.op("pool", lambda e: e.memset(Hf.ap[:], 0.0), writes=Hf.b)
        sch.op("pool", lambda e: e.memset(Hb.ap[:], 0.0), writes=Hb.b)
        for q in range(2):
            sch.op("pool", lambda e, q=q: e.memset(UV[q].ap[:], 0.0), writes=UV[q].b)
        gi0 = self.gi
        bif = [self.buf("ident_f")]
        mAT, mMN, bdm = self.mAT, self.mMN, self.bdm
        bank = self.bank
        NG = self.NG

        def prefetch(g):
            sl = (gi0 + g) % 2
            self.load_x(base, bname, g, sl)
            sch.dma("sp", "arld%d" % sl, lambda e, o=ARg[sl].ap[:], i=D["AR"][g]: e.dma_start(out=o, in_=i),
                    reads=[self.dregion("AR", g)], writes=ARg[sl].b)
            sch.dma("sp", "bkld%d" % sl, lambda e, o=BKg[sl].ap[:], i=D["BK"][g]: e.dma_start(out=o, in_=i),
                    reads=[self.dregion("BK", g)], writes=BKg[sl].b)
            sch.dma("sp", "glld%d" % sl, lambda e, o=GLg[sl].ap[:], i=D["GL"][g]: e.dma_start(out=o, in_=i),
                    reads=[self.dregion("GL", g)], writes=GLg[sl].b)

        def idx(h):
            return (h % 2) * 8 + h // 2

        def inv_steps(gc):
            g, ch = gc // NCH, gc % NCH
            q = ch % 2
            slot = (gi0 + g) % 2
            AR_, BK_ = ARg[slot], BKg[slot]
            AT, PT = ATs[gc % 2], PTs[gc % 2]
            bb_ = (1 - q) * 64
            steps = []

            def s_at(rnd):
                for hh in range(8):
                    h = rnd * 8 + hh
                    pr, po, i_ = h // 2, (h % 2) * 64, idx(h)
                    bk = bank[h % 2]
                    cs = (pr % 4) * 128
                    sch.op("pe", lambda e, pr=pr, po=po, bk=bk, cs=cs: e.matmul(
                        bk.ap[:, cs:cs + 128], lhsT=BK_.ap[po:po + 64, pr, ch, :],
                        rhs=AR_.ap[po:po + 64, pr, ch, :], start=True, stop=True),
                        reads=BK_.b + AR_.b, writes=bk.b)
                for par in range(2):
                    i0 = par * 8 + rnd * 4
                    sch.op("dve", lambda e, par=par, i0=i0: e.tensor_tensor(
                        out=AT.ap[:, i0:i0 + 4, :], in0=bank[par].ap[:].rearrange("p (h c) -> p h c", h=4),
                        in1=mAT.ap[:, q, :].unsqueeze(1).broadcast_to([128, 4, 128]), op=ALU.mult),
                        reads=bank[par].b + mAT.b, writes=AT.b)

            def s_nm(which):
                for h in range(16):
                    pr, po, i_ = h // 2, (h % 2) * 64, idx(h)
                    bk = bank[i_ // 8]
                    a_op = AR_.ap[po:po + 64, pr, ch, 0:64]
                    b_op = BK_.ap[po:po + 64, pr, ch, bb_:bb_ + 64]
                    l_, r_ = (a_op, b_op) if which == 0 else (b_op, a_op)
                    sch.op("pe", lambda e, i_=i_, bk=bk, l_=l_, r_=r_: e.matmul(
                        bk.ap[0:64, (i_ % 8) * 64:(i_ % 8 + 1) * 64], lhsT=l_, rhs=r_, start=True, stop=True),
                        reads=BK_.b + AR_.b, writes=bk.b)
                dstT = Ns[0] if which == 0 else Ms[0]
                mk = mMN.ap[:, 1, :] if which == 0 else mMN.ap[:, 0, :]
                for b2 in range(2):
                    sch.op("dve", lambda e, b2=b2, dstT=dstT, mk=mk: e.tensor_tensor(
                        out=dstT.ap[:, b2 * 8:(b2 + 1) * 8, :],
                        in0=bank[b2].ap[0:64, :].rearrange("p (h c) -> p h c", h=8),
                        in1=mk.unsqueeze(1).broadcast_to([64, 8, 64]), op=ALU.mult),
                        reads=bank[b2].b + mMN.b, writes=dstT.b)
                if which == 1:
                    sch.op("pool", lambda e: e.tensor_tensor(out=PT.ap[:], in0=Ms[0].ap[:],
                                                             in1=idf64.unsqueeze(1).broadcast_to([64, 16, 64]), op=ALU.add),
                           reads=Ms[0].b + bif, writes=PT.b)

            def s_sq(lev, kind):
                cur = lev % 2
                Mo, No, Mn, Nn = Ms[cur], Ns[cur], Ms[1 - cur], Ns[1 - cur]
                for i_ in range(16):
                    if kind == 0:
                        l_, r_ = No.ap[:, i_, :], Mo.ap[:, i_, :]
                        rd = Mo.b + No.b
                    elif kind == 1:
                        l_, r_ = Mo.ap[:, i_, :], No.ap[:, i_, :]
                        rd = Mo.b + No.b
                    else:
                        l_, r_ = Nn.ap[:, i_, :], PT.ap[:, i_, :]
                        rd = Nn.b + PT.b
                    sch.op("pe", lambda e, i_=i_, l_=l_, r_=r_: e.matmul(
                        bank[i_ // 8].ap[0:64, (i_ % 8) * 64:(i_ % 8 + 1) * 64], lhsT=l_, rhs=r_, start=True, stop=True),
                        reads=rd, writes=bank[i_ // 8].b)
                for b2 in range(2):
                    if kind == 0:
                        sch.op("act", lambda e, b2=b2, Mn=Mn: e.activation(
                            out=Mn.ap[:, b2 * 8:(b2 + 1) * 8, :].rearrange("p h c -> p (h c)"), in_=bank[b2].ap[0:64, :],
                            func=AF.Copy), reads=bank[b2].b, writes=Mn.b)
                    elif kind == 1:
                        sch.op("dve", lambda e, b2=b2, Nn=Nn: e.tensor_copy(
                            out=Nn.ap[:, b2 * 8:(b2 + 1) * 8, :].rearrange("p h c -> p (h c)"), in_=bank[b2].ap[0:64, :]),
                            reads=bank[b2].b, writes=Nn.b)
                    else:
                        sch.op("dve", lambda e, b2=b2: e.tensor_tensor(
                            out=PT.ap[:, b2 * 8:(b2 + 1) * 8, :].rearrange("p h c -> p (h c)"), in0=bank[b2].ap[0:64, :],
                            in1=PT.ap[:, b2 * 8:(b2 + 1) * 8, :].rearrange("p h c -> p (h c)"), op=ALU.add),
                            reads=bank[b2].b + PT.b, writes=PT.b)

            steps.append(lambda: s_at(0))
            steps.append(lambda: s_at(1))
            steps.append(lambda: s_nm(0))
            steps.append(lambda: s_nm(1))
            for lev in range(5):
                steps.append(lambda lev=lev: s_sq(lev, 1))
                if lev < 4:
                    steps.append(lambda lev=lev: s_sq(lev, 0))
                steps.append(lambda lev=lev: s_sq(lev, 2))
            return steps

        carry = {}

        def state_steps(gc):
            g, ch = gc // NCH, gc % NCH
            q, tt = ch % 2, ch // 2
            slot = (gi0 + g) % 2
            AR_, BK_, GL_ = ARg[slot], BKg[slot], GLg[slot]
            AT, PT = ATs[gc % 2], PTs[gc % 2]
            uv = UV[q]
            kb, bb_ = q * 64, (1 - q) * 64
            tok0 = g * G + ch * LCH
            steps = []

            def s_load_w():
                sch.dma("sp", "vld%d" % q, lambda e: e.dma_start(
                    out=uv.ap[q * 64:q * 64 + 64, :, :].rearrange("p h i -> p (h i)"), in_=D["V"][tok0:tok0 + LCH, :]),
                    reads=[self.dregion("V", g)], writes=uv.b)
                if q == 0:
                    gt, bo_ = gts[tt % 2], bons[tt % 2]
                    carry["gt"], carry["bo"] = gt, bo_
                    sch.dma("sp", "gld%d" % (tt % 2), lambda e: e.dma_start(
                        out=gt.ap[:], in_=D["Gt"][tok0:tok0 + 128, :]), reads=[self.dregion("Gt", g)], writes=gt.b)
                    sch.dma("sp", "bld%d" % (tt % 2), lambda e: e.dma_start(
                        out=bo_.ap[:], in_=D["BON"][tok0:tok0 + 128, :]), reads=[self.dregion("BON", g)], writes=bo_.b)
                for pr in range(KC):
                    bk = bank[2 + pr // 4]
                    c0 = (pr % 4) * 128
                    sch.op("pe", lambda e, pr=pr, bk=bk, c0=c0: e.matmul(
                        bk.ap[0:64, c0:c0 + 128], lhsT=AR_.ap[:, pr, ch, 0:64], rhs=Hb.ap[:, pr, :],
                        start=True, stop=False), reads=AR_.b + Hb.b, writes=bk.b)
                    for hp in range(2):
                        h = 2 * pr + hp
                        sch.op("pe", lambda e, h=h, hp=hp, bk=bk, c0=c0: e.matmul(
                            bk.ap[0:64, c0 + hp * 64:c0 + hp * 64 + 64], lhsT=AT.ap[:, idx(h), 0:64], rhs=uv.ap[:, h, :],
                            start=False, stop=(hp == 1)), reads=AT.b + uv.b, writes=bk.b)
                for b2 in range(2):
                    sch.op("act", lambda e, b2=b2: e.activation(
                        out=Wsb.ap[:, b2 * 8:(b2 + 1) * 8, :].rearrange("p h c -> p (h c)"), in_=bank[2 + b2].ap[0:64, :],
                        func=AF.Copy), reads=bank[2 + b2].b, writes=Wsb.b)

            def s_u():
                for h in range(16):
                    bk = bank[2 + h // 8]
                    sch.op("pe", lambda e, h=h, bk=bk: e.matmul(
                        bk.ap[bb_:bb_ + 64, (h % 8) * 64:(h % 8 + 1) * 64], lhsT=PT.ap[:, idx(h), :], rhs=Wsb.ap[:, h, :],
                        start=True, stop=True), reads=PT.b + Wsb.b, writes=bk.b)
                for b2 in range(2):
                    sch.op("dve", lambda e, b2=b2: e.tensor_copy(
                        out=uv.ap[bb_:bb_ + 64, b2 * 8:(b2 + 1) * 8, :].rearrange("p h c -> p (h c)"),
                        in_=bank[2 + b2].ap[bb_:bb_ + 64, :]), reads=bank[2 + b2].b, writes=uv.b)

            def s_y():
                for pr in range(KC):
                    bk = bank[4 + pr // 4]
                    c0 = (pr % 4) * 128
                    sch.op("pe", lambda e, pr=pr, bk=bk, c0=c0: e.matmul(
                        bk.ap[kb:kb + 64, c0:c0 + 128], lhsT=AR_.ap[:, pr, ch, 64:128], rhs=Hb.ap[:, pr, :],
                        start=True, stop=False), reads=AR_.b + Hb.b, writes=bk.b)
                    for hp in range(2):
                        h = 2 * pr + hp
                        sch.op("pe", lambda e, h=h, hp=hp, bk=bk, c0=c0: e.matmul(
                            bk.ap[kb:kb + 64, c0 + hp * 64:c0 + hp * 64 + 64], lhsT=AT.ap[:, idx(h), 64:128],
                            rhs=uv.ap[:, h, :], start=False, stop=(hp == 1)), reads=AT.b + uv.b, writes=bk.b)
                for pr in range(KC):
                    sch.op("pe", lambda e, pr=pr: e.transpose(
                        out=self.bankT[pr // 4].ap[:, (pr % 4) * 128:(pr % 4 + 1) * 128], in_=BK_.ap[:, pr, ch, :],
                        identity=self.ident[:]), reads=BK_.b + [self.bident], writes=self.bankT[pr // 4].b)
                for b2 in range(2):
                    sch.op("act", lambda e, b2=b2: e.activation(
                        out=BKtm.ap[:, b2 * 4:(b2 + 1) * 4, :].rearrange("p h c -> p (h c)"), in_=self.bankT[b2].ap[:, 0:512],
                        func=AF.Copy), reads=self.bankT[b2].b, writes=BKtm.b)

            def s_state():
                for b2 in range(2):
                    bk = bank[2 + b2]
                    for p4 in range(4):
                        pr = b2 * 4 + p4
                        sch.op("pe", lambda e, pr=pr, p4=p4, bk=bk: e.matmul(
                            bk.ap[:, p4 * 128:(p4 + 1) * 128], lhsT=BKtm.ap[:, pr, :],
                            rhs=uv.ap[:, 2 * pr:2 * pr + 2, :].rearrange("p h c -> p (h c)"), start=True, stop=True),
                            reads=BKtm.b + uv.b, writes=bk.b)
                    hs = slice(b2 * 4, b2 * 4 + 4)
                    sch.op("dve", lambda e, bk=bk: e.tensor_tensor(
                        out=bk.ap[:].rearrange("p (h c) -> p h c", h=4), in0=bk.ap[:].rearrange("p (h c) -> p h c", h=4),
                        in1=bdm.ap[:].unsqueeze(1).broadcast_to([128, 4, 128]), op=ALU.mult),
                        reads=bk.b + bdm.b, writes=bk.b)
                    sch.op("dve", lambda e, hs=hs, bk=bk: e.tensor_tensor(
                        out=Hf.ap[:, hs, :].rearrange("p h c -> p (h c)"), in0=bk.ap[:],
                        in1=Hf.ap[:, hs, :].rearrange("p h c -> p (h c)"), op=ALU.add), reads=bk.b + Hf.b, writes=Hf.b)
                    sch.op("dve", lambda e, hs=hs: e.tensor_tensor(
                        out=Hf.ap[:, hs, :], in0=Hf.ap[:, hs, :], in1=GL_.ap[:, hs, ch:ch + 1].broadcast_to([128, 4, 128]),
                        op=ALU.mult), reads=Hf.b + GL_.b, writes=Hf.b)
                    sch.op("act", lambda e, hs=hs: e.activation(out=Hb.ap[:, hs, :], in_=Hf.ap[:, hs, :], func=AF.Copy),
                           reads=Hf.b, writes=Hb.b)

            def s_post1():
                for b2 in range(2):
                    sch.op("act", lambda e, b2=b2: e.activation(
                        out=ysb.ap[:, b2 * 8:(b2 + 1) * 8, :].rearrange("p h c -> p (h c)"), in_=bank[4 + b2].ap[:],
                        func=AF.Copy), reads=bank[4 + b2].b, writes=ysb.b)
                sch.op("act", lambda e: e.activation(out=ysq.ap[:], in_=ysb.ap[:], func=AF.Square), reads=ysb.b, writes=ysq.b)
                sch.op("dve", lambda e: e.tensor_reduce(out=st.ap[:, 0, :], in_=ysb.ap[:], axis=AX.X, op=ALU.add),
                       reads=ysb.b, writes=st.b)
                sch.op("dve", lambda e: e.tensor_reduce(out=st.ap[:, 1, :], in_=ysq.ap[:], axis=AX.X, op=ALU.add),
                       reads=ysq.b, writes=st.b)
                sch.op("dve", lambda e: e.tensor_scalar(out=st.ap[:, 2, :], in0=st.ap[:, 0, :], scalar1=1.0 / 64, scalar2=None,
                                                        op0=ALU.mult), reads=st.b, writes=st.b)
                sch.op("dve", lambda e: e.tensor_tensor(out=st.ap[:, 0, :], in0=st.ap[:, 2, :], in1=st.ap[:, 2, :],
                                                        op=ALU.mult), reads=st.b, writes=st.b)
                sch.op("dve", lambda e: e.scalar_tensor_tensor(out=st.ap[:, 3, :], in0=st.ap[:, 1, :], scalar=1.0 / 64,
                                                               in1=st.ap[:, 0, :], op0=ALU.mult, op1=ALU.subtract),
                       reads=st.b, writes=st.b)
                sch.op("act", lambda e: e.activation(out=st.ap[:, 3, :], in_=st.ap[:, 3, :], func=AF.Sqrt, bias=float(GN_EPS)),
                       reads=st.b, writes=st.b)
                sch.op("dve", lambda e: e.reciprocal(out=st.ap[:, 3, :], in_=st.ap[:, 3, :]), reads=st.b, writes=st.b)

            def s_post2():
                gt, bo_ = carry["gt"], carry["bo"]
                b16 = lambda a_: a_.unsqueeze(2).broadcast_to([128, 16, 64])
                sch.op("dve", lambda e: e.tensor_tensor(out=ysb.ap[:], in0=ysb.ap[:], in1=b16(st.ap[:, 2, :]), op=ALU.subtract),
                       reads=ysb.b + st.b, writes=ysb.b)
                sch.op("dve", lambda e: e.tensor_tensor(out=ysb.ap[:], in0=ysb.ap[:], in1=b16(st.ap[:, 3, :]), op=ALU.mult),
                       reads=ysb.b + st.b, writes=ysb.b)
                sch.op("pool", lambda e: e.tensor_tensor(out=ysb.ap[:], in0=ysb.ap[:], in1=lnw.ap[:], op=ALU.mult),
                       reads=ysb.b + lnw.b, writes=ysb.b)
                sch.op("pool", lambda e: e.tensor_tensor(out=ysb.ap[:], in0=ysb.ap[:], in1=lnb.ap[:], op=ALU.add),
                       reads=ysb.b + lnb.b, writes=ysb.b)
                for qq in range(2):
                    rows = slice(qq * 64, qq * 64 + 64)
                    sch.op("pool", lambda e, qq=qq, rows=rows: e.tensor_tensor(
                        out=ysq.ap[rows, :, :], in0=UV[qq].ap[rows, :, :],
                        in1=bo_.ap[rows, :].unsqueeze(2).broadcast_to([64, 16, 64]), op=ALU.mult),
                        reads=UV[qq].b + bo_.b, writes=ysq.b)
                sch.op("dve", lambda e: e.tensor_tensor(out=ysb.ap[:], in0=ysb.ap[:], in1=ysq.ap[:], op=ALU.add),
                       reads=ysb.b + ysq.b, writes=ysb.b)
                sch.op("dve", lambda e: e.tensor_tensor(out=yg.ap[:], in0=ysb.ap[:].rearrange("p h c -> p (h c)"),
                                                        in1=gt.ap[:], op=ALU.mult), reads=ysb.b + gt.b, writes=yg.b)
                for c in range(KC):
                    sch.op("pe", lambda e, c=c: e.transpose(
                        out=self.bankT[c // 4].ap[:, (c % 4) * 128:(c % 4 + 1) * 128], in_=yg.ap[:, c * 128:(c + 1) * 128],
                        identity=self.ident[:]), reads=yg.b + [self.bident], writes=self.bankT[c // 4].b)
                for b2 in range(2):
                    sch.op("act", lambda e, b2=b2: e.activation(
                        out=ygT.ap[:, b2 * 4:(b2 + 1) * 4, tt * 128:(tt + 1) * 128],
                        in_=self.bankT[b2].ap[:, 0:512].rearrange("p (c t) -> p c t", c=4), func=AF.Copy),
                        reads=self.bankT[b2].b, writes=ygT.b)

            def s_group_end():
                self.out_proj(ygT, KC, wo, 1.0, slot)
                carry["tok"] = self.store_x(dst, dname, g, slot)

            steps += [s_load_w, s_u, s_y, s_state]
            if q == 1:
                steps += [s_post1, s_post2]
            if ch == NCH - 1:
                steps.append(s_group_end)
            return steps

        prefetch(0)
        ntot = NG * NCH
        for st_ in inv_steps(0):
            st_()
        for gc in range(ntot):
            g, ch = gc // NCH, gc % NCH
            if ch == 0 and g + 1 < NG:
                prefetch(g + 1)
            A = state_steps(gc)
            Bq = inv_steps(gc + 1) if gc + 1 < ntot else []
            i = j = 0
            while i < len(A) or j < len(Bq):
                for _ in range(3):
                    if j < len(Bq):
                        Bq[j]()
                        j += 1
                if i < len(A):
                    A[i]()
                    i += 1
        self.gi += NG
        return carry.get("tok")


WEIGHT_SHAPES = {
    "mem_w_kv": [C, 2 * C], "ffn1_w_in": [2, C, 2 * DFF], "ffn1_w_out": [2, DFF, C],
    "xattn_wq": [2, C, C], "xattn_wo": [2, C, C], "ffn2_w_in": [2, C, 2 * DFF], "ffn2_w_out": [2, DFF, C],
    "rwkv_w_rkv": [1, C, 3 * C], "rwkv_w1": [1, C, 64], "rwkv_w2": [1, 64, C], "rwkv_a1": [1, C, 64],
    "rwkv_a2": [1, 64, C], "rwkv_g1": [1, C, 160], "rwkv_g2": [1, 160, C], "rwkv_w_o": [1, C, C],
    "gmlp_w_uv": [1, C, 4096], "gmlp_w_o": [1, 2048, C],
    "wsT": [128, 16 * 128], "bsT": [128, 16 * 128], "lnw_bc": [128, C], "lnb_bc": [128, C], "fin_bc": [128, C],
    "mem": [NMEM, C],
}
MODE_INPUTS = {
    "ffn_test": ["ffn1_w_in", "ffn1_w_out"],
    "xattn_test": ["mem", "mem_w_kv", "xattn_wq", "xattn_wo"],
    "gmlp_test": ["gmlp_w_uv", "gmlp_w_o", "wsT", "bsT"],
    "rwkv_test": ["rwkv_w_rkv", "rwkv_w1", "rwkv_w2", "rwkv_a1", "rwkv_a2", "rwkv_g1", "rwkv_g2", "rwkv_w_o",
                  "lnw_bc", "lnb_bc"],
    "final_test": ["fin_bc"],
    "memkv_test": ["fin_bc", "mem", "mem_w_kv"],
}
MODE_INPUTS["full"] = sorted(set(sum([v for k, v in MODE_INPUTS.items() if k != "memkv_test"], [])
                                 + ["ffn2_w_in", "ffn2_w_out"]))


def build_program(S=4096, mode="full"):
    P = Prog(S)
    x = P.din("x", [S, C])
    out = P.dout("out", [S, C])
    P.setup()
    d = {n: P.din(n, WEIGHT_SHAPES[n]) for n in MODE_INPUTS[mode]}
    NG = P.NG
    P.hsave = P.dscratch("hsave", [NG, 128, KC, G], BF16)
    A = (P.dscratch("sA", [S, C]), "A")
    B = (P.dscratch("sB", [S, C]), "B")
    Cc = (P.dscratch("sC", [S, C]), "Cc")
    X = (x, "x")
    O = (out, "out")
    if mode in ("rwkv_test", "full"):
        P.rwkv_consts()
        D = {"AR": P.dscratch("dAR", [NG, 128, KC, NCH, 128], BF16),
             "BK": P.dscratch("dBK", [NG, 128, KC, NCH, 128], BF16),
             "V": P.dscratch("dV", [S, C], BF16), "Gt": P.dscratch("dGt", [S, C], BF16),
             "BON": P.dscratch("dBON", [S, 16]), "GL": P.dscratch("dGL", [NG, 128, KC, NCH])}
        RW = {"w_rkv": d["rwkv_w_rkv"], "w1": d["rwkv_w1"], "w2": d["rwkv_w2"], "a1": d["rwkv_a1"],
              "a2": d["rwkv_a2"], "g1": d["rwkv_g1"], "g2": d["rwkv_g2"]}

    def rwkv(src, dst):
        P.rwkv_r1(RW, src[0], src[1], D)
        return P.rwkv_r2(d["rwkv_w_o"], d["lnw_bc"], d["lnb_bc"], D, src[0], src[1], dst[0], dst[1])

    if mode == "ffn_test":
        P.ffn(d["ffn1_w_in"], d["ffn1_w_out"], 0, x, "x", A, B, out, "out", COLS["ffn1_norm0"])
    elif mode == "xattn_test":
        P.memkv(d["mem"], d["mem_w_kv"])
        P.xattn(d["xattn_wq"], d["xattn_wo"], 0, x, "x", out, "out", COLS["xattn_norm0"])
    elif mode == "gmlp_test":
        P.gmlp(d["gmlp_w_uv"], d["wsT"], d["bsT"], d["gmlp_w_o"], x, "x", out, "out", COLS["mix_norm1"])
    elif mode == "rwkv_test":
        rwkv(X, O)
    elif mode == "final_test":
        P.final_norm(d["fin_bc"], x, "x", out, "out")
    elif mode == "memkv_test":
        P.memkv(d["mem"], d["mem_w_kv"])
        P.final_norm(d["fin_bc"], x, "x", out, "out")
    else:
        P.memkv(d["mem"], d["mem_w_kv"])
        P.ffn(d["ffn1_w_in"], d["ffn1_w_out"], 0, x, "x", A, B, Cc[0], Cc[1], COLS["ffn1_norm0"])
        rwkv(Cc, A)
        P.xattn(d["xattn_wq"], d["xattn_wo"], 0, A[0], A[1], B[0], B[1], COLS["xattn_norm0"])
        P.ffn(d["ffn2_w_in"], d["ffn2_w_out"], 0, B[0], B[1], A, Cc, A[0], A[1], COLS["ffn2_norm0"])
        P.ffn(d["ffn1_w_in"], d["ffn1_w_out"], 1, A[0], A[1], B, Cc, B[0], B[1], COLS["ffn1_norm1"])
        P.gmlp(d["gmlp_w_uv"], d["wsT"], d["bsT"], d["gmlp_w_o"], B[0], B[1], A[0], A[1], COLS["mix_norm1"])
        P.xattn(d["xattn_wq"], d["xattn_wo"], 1, A[0], A[1], B[0], B[1], COLS["xattn_norm1"])
        P.ffn(d["ffn2_w_in"], d["ffn2_w_out"], 1, B[0], B[1], A, Cc, A[0], A[1], COLS["ffn2_norm1"])
        P.final_norm(d["fin_bc"], A[0], A[1], out, "out")
    finals = [s[2] for n, s in P.sch.dma_sems.items() if n.startswith("xst") and s[2] is not None]
    P.sch.run(finals)
    return P


def make_cols(inp):
    cols = np.zeros((128, NCOLS), np.float32)

    def put(name, v):
        v = np.asarray(v, np.float32).reshape(-1, 128)
        cols[:, COLS[name]:COLS[name] + v.shape[0]] = v.T

    for l in range(2):
        put("ffn1_norm%d" % l, inp["ffn1_norm"][l])
        put("mix_norm%d" % l, inp["mix_norm"][l])
        put("xattn_norm%d" % l, inp["xattn_norm"][l])
        put("ffn2_norm%d" % l, inp["ffn2_norm"][l])
    put("mem_norm", inp["mem_norm"])
    put("final_norm", inp["final_norm"])
    put("mu", inp["rwkv_mu"][0])
    put("w0", inp["rwkv_w0"][0])
    put("a0", inp["rwkv_a0"][0])
    put("k_k", inp["rwkv_k_k"][0])
    put("k_a", inp["rwkv_k_a"][0])
    put("r_k", inp["rwkv_r_k"][0])
    put("gv_norm", inp["gmlp_v_norm"][0])
    return cols


def host_layout(inp, names):
    f = lambda a: np.ascontiguousarray(np.asarray(a, np.float32))
    rep = lambda v: f(np.broadcast_to(np.asarray(v, np.float32).reshape(1, -1), (128, np.asarray(v).size)))
    out = {}
    for n in names:
        if n == "wsT":
            out[n] = f(np.transpose(np.asarray(inp["gmlp_w_s"][0], np.float32), (2, 0, 1)).reshape(128, 16 * 128))
        elif n == "bsT":
            out[n] = rep(inp["gmlp_b_s"][0])
        elif n == "lnw_bc":
            out[n] = rep(inp["rwkv_ln_w"][0])
        elif n == "lnb_bc":
            out[n] = rep(inp["rwkv_ln_b"][0])
        elif n == "fin_bc":
            out[n] = rep(inp["final_norm"])
        elif n == "mem":
            continue
        else:
            out[n] = f(inp[n])
    return out


_PROG_CACHE = {}


def kernel(**inputs):
    S = 4096
    if "full" not in _PROG_CACHE:
        _PROG_CACHE["full"] = build_program(S, "full")
    P = _PROG_CACHE["full"]
    shared = host_layout(inputs, MODE_INPUTS["full"])
    shared["cols"] = make_cols(inputs)
    x = np.asarray(inputs["x"], np.float32)
    mem = np.asarray(inputs["mem"], np.float32)
    in_maps = []
    for b in range(8):
        m = dict(shared)
        m["x"] = np.ascontiguousarray(x[b])
        m["mem"] = np.ascontiguousarray(mem[b])
        in_maps.append(m)
    res = run_bass_kernel_spmd(P.nc, in_maps, core_ids=list(range(8)))
    return np.stack([res.results[b]["out"] for b in range(8)], axis=0).astype(np.float32)
```

```python
import numpy as np
from contextlib import ExitStack
import concourse.bass as bass
import concourse.mybir as mybir
from concourse.bass_utils import run_bass_kernel_spmd

F32 = mybir.dt.float32
BF16 = mybir.dt.bfloat16
AF = mybir.ActivationFunctionType
ALU = mybir.AluOpType
AX = mybir.AxisListType

C = 1024
KC = C // 128
DFF = 2816
NMEM = 256
RMS_EPS = 1e-6
GN_EPS = 64e-5
SEM_EPOCH = 2000


class Tok:
    __slots__ = ("eng", "sem", "val", "needed", "is_dma", "idx")

    def __init__(self, eng, is_dma=False):
        self.eng = eng
        self.sem = None
        self.val = None
        self.needed = False
        self.is_dma = is_dma


class Buf:
    __slots__ = ("name", "writers", "readers")

    def __init__(self, name):
        self.name = name
        self.writers = {}
        self.readers = {}


class Sched:
    ENGS = ("pe", "act", "dve", "pool", "sp")

    def __init__(self, nc, stack):
        self.nc = nc
        self.stack = stack
        self.ops = {e: [] for e in self.ENGS}
        self.dma_sems = {}
        self.nsem = 0

    def new_sem(self, name):
        self.nsem += 1
        return self.stack.enter_context(self.nc.semaphore(name))

    def _deps(self, eng, reads, writes, is_dma):
        deps = []
        for b in reads:
            for k, t in b.writers.items():
                deps.append(t)
        for b in writes:
            for k, t in b.writers.items():
                if t.eng == eng and not t.is_dma and not is_dma and eng == "pe":
                    continue
                deps.append(t)
            for k, t in b.readers.items():
                if t.eng == eng and not t.is_dma and not is_dma and eng == "pe":
                    continue
                deps.append(t)
        return deps

    def _commit(self, tok, key, reads, writes):
        for b in reads:
            b.readers[key] = tok
        for b in writes:
            b.writers = {key: tok}
            b.readers = {}

    def op(self, eng, fn, reads=(), writes=()):
        deps = self._deps(eng, reads, writes, False)
        tok = Tok(eng)
        for d in deps:
            if d.eng == eng and not d.is_dma and eng == "pe":
                continue
            d.needed = True
        self.ops[eng].append((fn, deps, tok))
        self._commit(tok, eng, reads, writes)
        return tok

    def dma(self, queue, semname, fn, reads=(), writes=()):
        if semname not in self.dma_sems:
            self.dma_sems[semname] = [self.new_sem("d_" + semname), 0, None]
        s = self.dma_sems[semname]
        deps = self._deps(queue, reads, writes, True)
        if s[2] is not None:
            deps.append(s[2])
        for d in deps:
            d.needed = True
        tok = Tok(queue, is_dma=True)
        s[1] += 16
        tok.sem = s[0]
        tok.val = s[1]
        s[2] = tok
        self.ops[queue].append((fn, deps, tok))
        self._commit(tok, "dma:" + semname, reads, writes)
        return tok

    def finalize(self):
        for e in self.ENGS:
            cnt = 0
            sem = None
            for fn, deps, tok in self.ops[e]:
                if tok.is_dma or not tok.needed:
                    continue
                if sem is None or cnt == SEM_EPOCH:
                    sem = self.new_sem("e_%s_%d" % (e, self.nsem))
                    cnt = 0
                cnt += 1
                tok.sem = sem
                tok.val = cnt

    def replay(self, eng, engine):
        waited = {}
        for fn, deps, tok in self.ops[eng]:
            for d in deps:
                if d.eng == eng and not d.is_dma and eng == "pe":
                    continue
                key = id(d.sem)
                if waited.get(key, 0) >= d.val:
                    continue
                engine.wait_ge(d.sem, d.val)
                waited[key] = d.val
            ins = fn(engine)
            if tok.is_dma:
                ins.then_inc(tok.sem, 16)
            elif tok.needed:
                ins.then_inc(tok.sem, 1)

    def run(self, final_toks):
        self.finalize()
        nc = self.nc
        with nc.Block() as block:
            @block.tensor
            def _(e):
                self.replay("pe", e)

            @block.scalar
            def _(e):
                self.replay("act", e)

            @block.vector
            def _(e):
                self.replay("dve", e)

            @block.gpsimd
            def _(e):
                self.replay("pool", e)

            @block.sync
            def _(e):
                self.replay("sp", e)
                for t in final_toks:
                    e.wait_ge(t.sem, t.val)


RING_PAGES = 70
PAGE = 1024
A32_PAGES = 20
PAGE32 = 256
G = 512
NT = G // 128
LCH = 64
NCH = G // LCH
DEC = 0.6065306597126334

COLS = {}
_c = 0
for _name, _n in [("ffn1_norm0", 8), ("ffn1_norm1", 8), ("mix_norm0", 8), ("mix_norm1", 8),
                  ("xattn_norm0", 8), ("xattn_norm1", 8), ("ffn2_norm0", 8), ("ffn2_norm1", 8),
                  ("mem_norm", 8), ("final_norm", 8), ("mu", 48), ("w0", 8), ("a0", 8),
                  ("k_k", 8), ("k_a", 8), ("r_k", 8), ("gv_norm", 16), ("omka", 8)]:
    COLS[_name] = _c
    _c += _n
NCOLS = _c


class T:
    __slots__ = ("ap", "b")

    def __init__(self, ap, bufs):
        self.ap = ap
        self.b = list(bufs)


def _shape_view(ap2d, shape):
    if len(shape) == 1:
        return ap2d
    if len(shape) == 2:
        return ap2d.rearrange("p (a b) -> p a b", a=shape[0])
    if len(shape) == 3:
        return ap2d.rearrange("p (a b c) -> p a b c", a=shape[0], b=shape[1])
    raise ValueError(shape)


class WView:
    def __init__(self, view, kc, kstep):
        self.v = view
        self.kc = kc
        self.kstep = kstep
        self.piece_bufs = []

    def b(self, k):
        return self.piece_bufs[k // self.kstep]

    def all(self):
        return [b for pb in self.piece_bufs for b in pb]


class Prog:
    def __init__(self, S=4096):
        self.S = S
        self.NG = S // G
        self.nc = bass.Bass("TRN2", target_bir_lowering=False)
        self.stack = ExitStack()
        self.sch = Sched(self.nc, self.stack)
        self.bufs = {}
        self.dram = {}
        self.ring_off = 0
        self.ring_used = 0
        self.a32_off = 0
        self.wrr = 0
        self.gi = 0
        self.final_toks = []

    def din(self, name, shape, dt=F32):
        t = self.nc.dram_tensor(name, list(shape), dt, kind="ExternalInput").ap()
        self.dram[name] = t
        return t

    def dout(self, name, shape, dt=F32):
        return self.nc.dram_tensor(name, list(shape), dt, kind="ExternalOutput").ap()

    def dscratch(self, name, shape, dt=F32):
        return self.nc.dram_tensor(name, list(shape), dt, kind="Internal").ap()

    def sb(self, name, shape, dt):
        return self.stack.enter_context(self.nc.sbuf_tensor(name, list(shape), dt))

    def ps(self, name, shape, dt=F32):
        return self.stack.enter_context(self.nc.psum_tensor(name, list(shape), dt))

    def buf(self, name):
        if name not in self.bufs:
            self.bufs[name] = Buf(name)
        return self.bufs[name]

    def fixed(self, name, shape, dt):
        return T(self.sb(name, shape, dt), [self.buf(name)])

    def phase_begin(self, reset=False):
        self.ring_used = 0
        self.a32_off = 0
        if reset:
            self.ring_off = 0

    def _ring(self, nelem):
        npg = (nelem + PAGE - 1) // PAGE
        p0 = self.ring_off
        if p0 + npg > RING_PAGES:
            self.ring_used += RING_PAGES - p0
            p0 = 0
        self.ring_off = p0 + npg
        self.ring_used += npg
        assert self.ring_used <= RING_PAGES, "phase over-subscribes the ring arena"
        return p0 * PAGE, [self.buf("pg%d" % p) for p in range(p0, p0 + npg)]

    def r16(self, shape, part=128):
        n = int(np.prod(shape))
        off, pgs = self._ring(n)
        return T(_shape_view(self.ring[0:part, off:off + n], shape), pgs)

    def a32(self, shape, part=128):
        n = int(np.prod(shape))
        npg = (n + PAGE32 - 1) // PAGE32
        p0 = self.a32_off
        self.a32_off += npg
        assert self.a32_off <= A32_PAGES, "fp32 scratch over-subscribed"
        return T(_shape_view(self.arena32[0:part, p0 * PAGE32:p0 * PAGE32 + n], shape),
                 [self.buf("fp%d" % p) for p in range(p0, p0 + npg)])

    def wload(self, src_ap, kc, n, part=128, kstep=4):
        if kc * n <= PAGE * 4:
            kstep = kc
        while kstep < kc and (kstep * n) % PAGE != 0:
            kstep += 1
        off, pgs = self._ring(kc * n)
        view = self.ring[0:part, off:off + kc * n].rearrange("p (k n) -> p k n", k=kc)
        w = WView(view, kc, kstep)
        for k0 in range(0, kc, kstep):
            k1 = min(kc, k0 + kstep)
            pb = pgs[(k0 * n) // PAGE:((k1 * n) + PAGE - 1) // PAGE]
            w.piece_bufs.append(pb)
            for n0 in range(0, n, 1024):
                n1 = min(n, n0 + 1024)
                self.wrr = (self.wrr + 1) % 12
                self.sch.dma("pool", "w%d" % self.wrr,
                             (lambda e, o=view[:, k0:k1, n0:n1], i=src_ap[:, k0:k1, n0:n1]:
                              e.dma_start(out=o, in_=i)),
                             reads=(), writes=pb)
        return w

    def setup(self):
        nc, sch = self.nc, self.sch
        self.ring = self.sb("ring", [128, RING_PAGES * PAGE], BF16)
        self.arena32 = self.sb("arena32", [128, A32_PAGES * PAGE32], F32)
        self.cols = self.sb("cols_sb", [128, NCOLS], F32)
        self.bcols = self.buf("cols")
        self.ident_f = self.sb("ident_f", [128, 128], F32)
        self.ident = self.sb("ident", [128, 128], BF16)
        self.bident = self.buf("ident")
        self.xt = [self.fixed("xt%d" % i, [128, NT, C], F32) for i in range(2)]
        self.ss = self.fixed("ss", [128, NT], F32)
        self.rs = self.fixed("rs", [128, NT], F32)
        self.KT = self.fixed("KT", [128, KC, NMEM], BF16)
        self.Vm = self.fixed("Vm", [128, 2, C], BF16)
        self.ones = self.fixed("ones", [128, 128], BF16)
        self.bank = [T(self.ps("bank%d" % i, [128, 512], F32), [self.buf("bank%d" % i)]) for i in range(6)]
        self.bankT = [T(self.ps("bankT%d" % i, [128, 1024], BF16), [self.buf("bankT%d" % i)]) for i in range(2)]
        cols_d = self.din("cols", [128, NCOLS])
        sch.dma("sp", "cols", lambda e: e.dma_start(out=self.cols[:], in_=cols_d[:, :]), writes=[self.bcols])
        bif = self.buf("ident_f")
        sch.op("pool", lambda e: e.memset(self.ident_f[:], 1.0), writes=[bif])
        sch.op("pool", lambda e: e.affine_select(out=self.ident_f[:], in_=self.ident_f[:],
                                                 pattern=[[-1, 128]], compare_op=ALU.is_equal,
                                                 fill=0.0, base=0, channel_multiplier=1),
               reads=[bif], writes=[bif])
        sch.op("dve", lambda e: e.tensor_copy(out=self.ident[:], in_=self.ident_f[:]),
               reads=[bif], writes=[self.bident])
        sch.op("pool", lambda e: e.memset(self.ones.ap[:], 1.0), writes=self.ones.b)
        ka, om = COLS["k_a"], COLS["omka"]
        sch.op("dve", lambda e: e.tensor_scalar(out=self.cols[:, om:om + 8], in0=self.cols[:, ka:ka + 8],
                                                scalar1=-1.0, scalar2=1.0, op0=ALU.mult, op1=ALU.add),
               reads=[self.bcols], writes=[self.bcols])

    def dregion(self, dname, g):
        return self.buf("dr_%s_%d" % (dname, g))

    def load_x(self, src, sname, g, slot):
        view = src[g * G:(g + 1) * G, :].rearrange("(t p) c -> p t c", p=128)
        return self.sch.dma("sp", "xld%d" % slot,
                            lambda e, o=self.xt[slot].ap[:], i=view: e.dma_start(out=o, in_=i),
                            reads=[self.dregion(sname, g)], writes=self.xt[slot].b)

    def store_x(self, dst, dname, g, slot):
        view = dst[g * G:(g + 1) * G, :].rearrange("(t p) c -> p t c", p=128)
        return self.sch.dma("sp", "xst%d" % slot,
                            lambda e, o=view, i=self.xt[slot].ap[:]: e.dma_start(out=o, in_=i),
                            reads=self.xt[slot].b, writes=[self.dregion(dname, g)])

    def norm_T(self, slot, hT, xn, sq, gain_col, eps=RMS_EPS, nt=NT, col0=0):
        sch = self.sch
        xt = self.xt[slot]
        ss, rs = self.ss, self.rs
        for t in range(nt):
            sch.op("act", lambda e, t=t: e.activation(out=sq.ap[:], in_=xt.ap[:, t, :], func=AF.Square,
                                                      accum_out=ss.ap[:, t:t + 1]),
                   reads=xt.b, writes=sq.b + ss.b)
        sch.op("act", lambda e: e.activation(out=rs.ap[:, 0:nt], in_=ss.ap[:, 0:nt], func=AF.Sqrt,
                                             bias=float(eps), scale=1.0 / C), reads=ss.b, writes=rs.b)
        sch.op("dve", lambda e: e.reciprocal(out=rs.ap[:, 0:nt], in_=rs.ap[:, 0:nt]), reads=rs.b, writes=rs.b)
        for t in range(nt):
            sch.op("dve", lambda e, t=t: e.tensor_scalar(out=xn.ap[:, t, :], in0=xt.ap[:, t, :],
                                                         scalar1=rs.ap[:, t:t + 1], scalar2=None, op0=ALU.mult),
                   reads=xt.b + rs.b, writes=xn.b)
        for c in range(KC):
            bT = self.bankT[c % 2]
            for t in range(nt):
                sch.op("pe", lambda e, c=c, t=t, bT=bT: e.transpose(
                    out=bT.ap[:, t * 128:(t + 1) * 128], in_=xn.ap[:, t, c * 128:(c + 1) * 128],
                    identity=self.ident[:]), reads=xn.b + [self.bident], writes=bT.b)
            sch.op("act", lambda e, c=c, bT=bT: e.activation(
                out=hT.ap[:, c, col0:col0 + nt * 128], in_=bT.ap[:, 0:nt * 128], func=AF.Copy,
                scale=self.cols[:, gain_col + c:gain_col + c + 1]),
                reads=bT.b + [self.bcols], writes=hT.b)

    def out_proj(self, act, nk, W, scale, slot):
        sch = self.sch
        xt = self.xt[slot]
        for t in range(NT):
            for nb in range(2):
                bk = self.bank[4 + (t * 2 + nb) % 2]
                for k in range(nk):
                    sch.op("pe", lambda e, k=k, t=t, nb=nb, bk=bk: e.matmul(
                        bk.ap[:], lhsT=act.ap[:, k, t * 128:(t + 1) * 128],
                        rhs=W.v[:, k, nb * 512:(nb + 1) * 512], start=(k == 0), stop=(k == nk - 1)),
                        reads=act.b + W.b(k), writes=bk.b)
                sch.op("dve", lambda e, t=t, nb=nb, bk=bk: e.scalar_tensor_tensor(
                    out=xt.ap[:, t, nb * 512:(nb + 1) * 512], in0=bk.ap[:], scalar=float(scale),
                    in1=xt.ap[:, t, nb * 512:(nb + 1) * 512], op0=ALU.mult, op1=ALU.add),
                    reads=bk.b + xt.b, writes=xt.b)

    def ffn_pass(self, w_in, w_out, l, j0, nj, base, bname, dst, dname, gain_col, first):
        sch = self.sch
        self.phase_begin()
        win = w_in[l].rearrange("(kc p) n -> p kc n", p=128)
        wgs, wus = [], []
        for jb in range(0, nj, 2):
            nb_ = min(2, nj - jb)
            wgs.append(self.wload(win[:, :, (j0 + jb) * 128:(j0 + jb + nb_) * 128], KC, nb_ * 128, kstep=KC))
            wus.append(self.wload(win[:, :, DFF + (j0 + jb) * 128:DFF + (j0 + jb + nb_) * 128], KC, nb_ * 128, kstep=KC))
        wo = self.wload(w_out[l].rearrange("(kc p) n -> p kc n", p=128)[:, j0:j0 + nj, :], nj, C)
        hTs = [self.r16([KC, G]) for _ in range(2)]
        hids = [self.r16([nj, G]) for _ in range(2)]
        sgs = [self.r16([G]) for _ in range(2)]
        if first:
            xn = self.r16([NT, C])
            sq = self.r16([C])
        hsave = self.hsave
        gi0 = self.gi

        def prefetch(g):
            slot = (gi0 + g) % 2
            self.load_x(base, bname, g, slot)
            if not first:
                sch.dma("sp", "hld%d" % slot,
                        lambda e, o=hTs[slot].ap[:], i=hsave[g]: e.dma_start(out=o, in_=i),
                        reads=[self.dregion("hsave", g)], writes=hTs[slot].b)

        prefetch(0)
        tok = None
        for g in range(self.NG):
            slot = (gi0 + g) % 2
            hT, hid = hTs[slot], hids[slot]
            if g + 1 < self.NG:
                prefetch(g + 1)
            if first:
                self.norm_T(slot, hT, xn, sq, gain_col)
                sch.dma("sp", "hst%d" % slot, lambda e, o=hsave[g], i=hT.ap[:]: e.dma_start(out=o, in_=i),
                        reads=hT.b, writes=[self.dregion("hsave", g)])
            for j in range(nj):
                pg, pu = self.bank[(j % 2) * 2], self.bank[(j % 2) * 2 + 1]
                wg, wu, jo = wgs[j // 2], wus[j // 2], (j % 2) * 128
                for k in range(KC):
                    sch.op("pe", lambda e, k=k, jo=jo, pg=pg, hT=hT, wg=wg: e.matmul(
                        pg.ap[:], lhsT=wg.v[:, k, jo:jo + 128], rhs=hT.ap[:, k, :],
                        start=(k == 0), stop=(k == KC - 1)), reads=hT.b + wg.b(k), writes=pg.b)
                for k in range(KC):
                    sch.op("pe", lambda e, k=k, jo=jo, pu=pu, hT=hT, wu=wu: e.matmul(
                        pu.ap[:], lhsT=wu.v[:, k, jo:jo + 128], rhs=hT.ap[:, k, :],
                        start=(k == 0), stop=(k == KC - 1)), reads=hT.b + wu.b(k), writes=pu.b)
                sg = sgs[j % 2]
                sch.op("act", lambda e, pg=pg, sg=sg: e.activation(out=sg.ap[:], in_=pg.ap[:], func=AF.Silu),
                       reads=pg.b, writes=sg.b)
                sch.op("dve", lambda e, j=j, pu=pu, sg=sg, hid=hid: e.tensor_tensor(
                    out=hid.ap[:, j, :], in0=pu.ap[:], in1=sg.ap[:], op=ALU.mult),
                    reads=pu.b + sg.b, writes=hid.b)
            self.out_proj(hid, nj, wo, 0.5, slot)
            tok = self.store_x(dst, dname, g, slot)
        self.gi += self.NG
        return tok

    def ffn(self, w_in, w_out, l, src, sname, tmpA, tmpB, dst, dname, gain_col):
        self.ffn_pass(w_in, w_out, l, 0, 8, src, sname, tmpA[0], tmpA[1], gain_col, True)
        self.ffn_pass(w_in, w_out, l, 8, 7, tmpA[0], tmpA[1], tmpB[0], tmpB[1], gain_col, False)
        return self.ffn_pass(w_in, w_out, l, 15, 7, tmpB[0], tmpB[1], dst, dname, gain_col, False)

    def memkv(self, mem, mem_w_kv):
        sch = self.sch
        self.phase_begin()
        w = self.wload(mem_w_kv.rearrange("(kc p) n -> p kc n", p=128), KC, 2 * C)
        hT = self.r16([KC, G])
        xn = self.r16([NT, C])
        sq = self.r16([C])
        slot = self.gi % 2
        self.gi += 1
        view = mem[:, :].rearrange("(t p) c -> p t c", p=128)
        sch.dma("sp", "xld%d" % slot, lambda e, o=self.xt[slot].ap[:, 0:2, :], i=view: e.dma_start(out=o, in_=i),
                writes=self.xt[slot].b)
        self.norm_T(slot, hT, xn, sq, COLS["mem_norm"], nt=2)
        for m in range(KC):
            bk = self.bank[m % 2]
            for k in range(KC):
                sch.op("pe", lambda e, k=k, m=m, bk=bk: e.matmul(
                    bk.ap[:, 0:NMEM], lhsT=w.v[:, k, m * 128:(m + 1) * 128], rhs=hT.ap[:, k, 0:NMEM],
                    start=(k == 0), stop=(k == KC - 1)), reads=hT.b + w.b(k), writes=bk.b)
            sch.op("act", lambda e, m=m, bk=bk: e.activation(out=self.KT.ap[:, m, :], in_=bk.ap[:, 0:NMEM],
                                                           func=AF.Copy), reads=bk.b, writes=self.KT.b)
        for mc in range(2):
            for nb in range(2):
                bk = self.bank[2 + (mc * 2 + nb) % 2]
                for k in range(KC):
                    sch.op("pe", lambda e, k=k, mc=mc, nb=nb, bk=bk: e.matmul(
                        bk.ap[:], lhsT=hT.ap[:, k, mc * 128:(mc + 1) * 128],
                        rhs=w.v[:, k, C + nb * 512:C + (nb + 1) * 512],
                        start=(k == 0), stop=(k == KC - 1)), reads=hT.b + w.b(k), writes=bk.b)
                sch.op("dve", lambda e, mc=mc, nb=nb, bk=bk: e.tensor_copy(
                    out=self.Vm.ap[:, mc, nb * 512:(nb + 1) * 512], in_=bk.ap[:]), reads=bk.b, writes=self.Vm.b)

    def xattn(self, wq_d, wo_d, l, src, sname, dst, dname, gain_col):
        sch = self.sch
        self.phase_begin()
        wq = self.wload(wq_d[l].rearrange("(kc p) n -> p kc n", p=128), KC, C)
        wo = self.wload(wo_d[l].rearrange("(kc p) n -> p kc n", p=128), KC, C)
        hT = self.r16([KC, G])
        xn = self.r16([NT, C])
        sq = self.r16([C])
        qT = self.r16([KC, G])
        oT = self.r16([KC, G])
        exs = [self.r16([2, G]) for _ in range(2)]
        rden = self.a32([G])
        KT, Vm = self.KT, self.Vm
        gi0 = self.gi
        self.load_x(src, sname, 0, gi0 % 2)
        tok = None
        for g in range(self.NG):
            slot = (gi0 + g) % 2
            if g + 1 < self.NG:
                self.load_x(src, sname, g + 1, (gi0 + g + 1) % 2)
            self.norm_T(slot, hT, xn, sq, gain_col)
            for m in range(KC):
                bk = self.bank[m % 2]
                for k in range(KC):
                    sch.op("pe", lambda e, k=k, m=m, bk=bk: e.matmul(
                        bk.ap[:], lhsT=wq.v[:, k, m * 128:(m + 1) * 128], rhs=hT.ap[:, k, :],
                        start=(k == 0), stop=(k == KC - 1)), reads=hT.b + wq.b(k), writes=bk.b)
                sch.op("act", lambda e, m=m, bk=bk: e.activation(out=qT.ap[:, m, :], in_=bk.ap[:], func=AF.Copy,
                                                               scale=1.0 / 16.0), reads=bk.b, writes=qT.b)
            for hh in range(4):
                ex = exs[hh % 2]
                for mc in range(2):
                    bk = self.bank[2 + mc]
                    for j in range(2):
                        sch.op("pe", lambda e, j=j, mc=mc, hh=hh, bk=bk: e.matmul(
                            bk.ap[:], lhsT=KT.ap[:, 2 * hh + j, mc * 128:(mc + 1) * 128], rhs=qT.ap[:, 2 * hh + j, :],
                            start=(j == 0), stop=(j == 1)), reads=KT.b + qT.b, writes=bk.b)
                    sch.op("act", lambda e, mc=mc, bk=bk, ex=ex: e.activation(out=ex.ap[:, mc, :], in_=bk.ap[:],
                                                                             func=AF.Exp), reads=bk.b, writes=ex.b)
                bd = self.bank[4]
                for mc in range(2):
                    sch.op("pe", lambda e, mc=mc, ex=ex, bd=bd: e.matmul(
                        bd.ap[:], lhsT=self.ones.ap[:], rhs=ex.ap[:, mc, :], start=(mc == 0), stop=(mc == 1)),
                        reads=self.ones.b + ex.b, writes=bd.b)
                sch.op("dve", lambda e, bd=bd: e.reciprocal(out=rden.ap[:], in_=bd.ap[:]), reads=bd.b, writes=rden.b)
                for j in range(2):
                    bo = self.bank[j]
                    for mc in range(2):
                        sch.op("pe", lambda e, mc=mc, j=j, hh=hh, ex=ex, bo=bo: e.matmul(
                            bo.ap[:], lhsT=Vm.ap[:, mc, (2 * hh + j) * 128:(2 * hh + j + 1) * 128], rhs=ex.ap[:, mc, :],
                            start=(mc == 0), stop=(mc == 1)), reads=Vm.b + ex.b, writes=bo.b)
                    sch.op("dve", lambda e, j=j, hh=hh, bo=bo: e.tensor_tensor(
                        out=oT.ap[:, 2 * hh + j, :], in0=bo.ap[:], in1=rden.ap[:], op=ALU.mult),
                        reads=bo.b + rden.b, writes=oT.b)
            self.out_proj(oT, KC, wo, 1.0, slot)
            tok = self.store_x(dst, dname, g, slot)
        self.gi += self.NG
        return tok

    def gmlp(self, w_uv, wsT_d, bsT_d, w_o, src, sname, dst, dname, gain_col, final=None):
        sch = self.sch
        self.phase_begin(reset=True)
        wuv = w_uv[0].rearrange("(kc p) n -> p kc n", p=128)
        wu = self.wload(wuv[:, :, 0:2048], KC, 2048)
        wv = self.wload(wuv[:, :, 2048:4096], KC, 2048)
        wo = self.wload(w_o[0].rearrange("(kc p) n -> p kc n", p=128), 16, C)
        ws = self.wload(wsT_d.rearrange("s (g t) -> s g t", g=16), 16, 128)
        sch.op("pool", lambda e: e.affine_select(out=ws.v[:], in_=ws.v[:], pattern=[[0, 16], [1, 128]],
                                                 compare_op=ALU.is_ge, fill=0.0, base=0, channel_multiplier=-1),
               reads=ws.all(), writes=ws.all())
        bsT = self.a32([16, 128])
        sch.dma("sp", "bsT", lambda e: e.dma_start(out=bsT.ap[:], in_=bsT_d.rearrange("p (g t) -> p g t", g=16)),
                writes=bsT.b)
        hT = self.r16([KC, G])
        vt = self.r16([NT, 2048])
        xn = T(vt.ap[:, 0:2, :].rearrange("p a (b c) -> p (a b) c", c=C), vt.b[0:4])
        uT = self.r16([16, G])
        sq = T(uT.ap[:, 0:4, :].rearrange("p a b -> p (a b)"), uT.b[0:4])
        svs = [self.a32([G]) for _ in range(2)]
        gvc = COLS["gv_norm"]
        ss, rs = self.ss, self.rs
        gi0 = self.gi
        self.load_x(src, sname, 0, gi0 % 2)
        tok = None
        for g in range(self.NG):
            slot = (gi0 + g) % 2
            if g + 1 < self.NG:
                self.load_x(src, sname, g + 1, (gi0 + g + 1) % 2)
            self.norm_T(slot, hT, xn, T(sq.ap[:, 0:C], sq.b), gain_col)
            for t in range(NT):
                for nb in range(4):
                    bk = self.bank[2 + nb % 2]
                    for k in range(KC):
                        sch.op("pe", lambda e, k=k, t=t, nb=nb, bk=bk: e.matmul(
                            bk.ap[:], lhsT=hT.ap[:, k, t * 128:(t + 1) * 128], rhs=wv.v[:, k, nb * 512:(nb + 1) * 512],
                            start=(k == 0), stop=(k == KC - 1)), reads=hT.b + wv.b(k), writes=bk.b)
                    sch.op("act", lambda e, t=t, nb=nb, bk=bk: e.activation(
                        out=vt.ap[:, t, nb * 512:(nb + 1) * 512], in_=bk.ap[:], func=AF.Gelu), reads=bk.b, writes=vt.b)
                sch.op("act", lambda e, t=t: e.activation(out=sq.ap[:], in_=vt.ap[:, t, :], func=AF.Square,
                                                          accum_out=ss.ap[:, t:t + 1]), reads=vt.b, writes=sq.b + ss.b)
            sch.op("act", lambda e: e.activation(out=rs.ap[:], in_=ss.ap[:], func=AF.Sqrt,
                                                 bias=float(RMS_EPS), scale=1.0 / 2048.0), reads=ss.b, writes=rs.b)
            sch.op("dve", lambda e: e.reciprocal(out=rs.ap[:], in_=rs.ap[:]), reads=rs.b, writes=rs.b)
            for t in range(NT):
                sch.op("dve", lambda e, t=t: e.tensor_scalar(out=vt.ap[:, t, :], in0=vt.ap[:, t, :],
                                                             scalar1=rs.ap[:, t:t + 1], scalar2=None, op0=ALU.mult),
                       reads=vt.b + rs.b, writes=vt.b)
            for gc in range(16):
                bk = self.bank[gc % 2]
                for k in range(KC):
                    sch.op("pe", lambda e, k=k, gc=gc, bk=bk: e.matmul(
                        bk.ap[:], lhsT=wu.v[:, k, gc * 128:(gc + 1) * 128], rhs=hT.ap[:, k, :],
                        start=(k == 0), stop=(k == KC - 1)), reads=hT.b + wu.b(k), writes=bk.b)
                sch.op("act", lambda e, gc=gc, bk=bk: e.activation(out=uT.ap[:, gc, :], in_=bk.ap[:], func=AF.Gelu),
                       reads=bk.b, writes=uT.b)
                bs_ = self.bank[3 + gc % 2]
                for t in range(NT):
                    sch.op("pe", lambda e, t=t, gc=gc, bs_=bs_: e.matmul(
                        bs_.ap[:, t * 128:(t + 1) * 128], lhsT=vt.ap[:, t, gc * 128:(gc + 1) * 128], rhs=ws.v[:, gc, :],
                        start=True, stop=True), reads=vt.b + ws.all(), writes=bs_.b)
                sv = svs[gc % 2]
                sch.op("dve", lambda e, gc=gc, bs_=bs_, sv=sv: e.scalar_tensor_tensor(
                    out=sv.ap[:].rearrange("p (t s) -> p t s", t=NT), in0=bs_.ap[:].rearrange("p (t s) -> p t s", t=NT),
                    scalar=self.cols[:, gvc + gc:gvc + gc + 1],
                    in1=bsT.ap[:, gc, :].unsqueeze(1).broadcast_to([128, NT, 128]),
                    op0=ALU.mult, op1=ALU.add), reads=bs_.b + [self.bcols] + bsT.b, writes=sv.b)
                sch.op("dve", lambda e, gc=gc, sv=sv: e.tensor_tensor(out=uT.ap[:, gc, :], in0=uT.ap[:, gc, :],
                                                                     in1=sv.ap[:], op=ALU.mult),
                       reads=uT.b + sv.b, writes=uT.b)
            self.out_proj(uT, 16, wo, 1.0, slot)
            tok = self.store_x(dst, dname, g, slot)
        self.gi += self.NG
        return tok

    def final_norm(self, gain_bc_d, src, sname, dst, dname):
        sch = self.sch
        self.phase_begin()
        gbc = self.a32([C])
        sch.dma("sp", "gbc", lambda e: e.dma_start(out=gbc.ap[:], in_=gain_bc_d[:, :]), writes=gbc.b)
        sq = self.r16([C])
        ss, rs = self.ss, self.rs
        gi0 = self.gi
        self.load_x(src, sname, 0, gi0 % 2)
        toks = []
        for g in range(self.NG):
            slot = (gi0 + g) % 2
            xt = self.xt[slot]
            if g + 1 < self.NG:
                self.load_x(src, sname, g + 1, (gi0 + g + 1) % 2)
            for t in range(NT):
                sch.op("act", lambda e, t=t, xt=xt: e.activation(out=sq.ap[:], in_=xt.ap[:, t, :], func=AF.Square,
                                                                 accum_out=ss.ap[:, t:t + 1]),
                       reads=xt.b, writes=sq.b + ss.b)
            sch.op("act", lambda e: e.activation(out=rs.ap[:], in_=ss.ap[:], func=AF.Sqrt,
                                                 bias=float(RMS_EPS), scale=1.0 / C), reads=ss.b, writes=rs.b)
            sch.op("dve", lambda e: e.reciprocal(out=rs.ap[:], in_=rs.ap[:]), reads=rs.b, writes=rs.b)
            for t in range(NT):
                sch.op("dve", lambda e, t=t, xt=xt: e.scalar_tensor_tensor(
                    out=xt.ap[:, t, :], in0=xt.ap[:, t, :], scalar=rs.ap[:, t:t + 1], in1=gbc.ap[:],
                    op0=ALU.mult, op1=ALU.mult), reads=xt.b + rs.b + gbc.b, writes=xt.b)
            toks.append(self.store_x(dst, dname, g, slot))
        self.gi += self.NG
        return toks

    def rwkv_consts(self):
        sch = self.sch
        self.oblk = self.fixed("oblk", [128, 128], BF16)
        self.hind = self.fixed("hind", [128, KC, 16], BF16)
        self.rmask = self.fixed("rmask", [128, G], BF16)
        self.st = self.fixed("st", [128, 4, 16], F32)
        self.mAT = self.fixed("mAT", [128, 2, 128], F32)
        self.mMN = self.fixed("mMN", [64, 2, 64], F32)
        self.bdm = self.fixed("bdm", [128, 128], F32)
        P = "pool"
        sch.op(P, lambda e: e.memset(self.oblk.ap[:], 0.0), writes=self.oblk.b)
        sch.op(P, lambda e: e.memset(self.oblk.ap[0:64, 0:64], 1.0), writes=self.oblk.b)
        sch.op(P, lambda e: e.memset(self.oblk.ap[64:128, 64:128], 1.0), writes=self.oblk.b)
        sch.op(P, lambda e: e.memset(self.bdm.ap[:], 0.0), writes=self.bdm.b)
        sch.op(P, lambda e: e.memset(self.bdm.ap[0:64, 0:64], 1.0), writes=self.bdm.b)
        sch.op(P, lambda e: e.memset(self.bdm.ap[64:128, 64:128], 1.0), writes=self.bdm.b)
        sch.op(P, lambda e: e.memset(self.hind.ap[:], 0.0), writes=self.hind.b)
        for c in range(KC):
            sch.op(P, lambda e, c=c: e.memset(self.hind.ap[0:64, c, 2 * c:2 * c + 1], 1.0), writes=self.hind.b)
            sch.op(P, lambda e, c=c: e.memset(self.hind.ap[64:128, c, 2 * c + 1:2 * c + 2], 1.0), writes=self.hind.b)
        sch.op(P, lambda e: e.memset(self.rmask.ap[:], 1.0), writes=self.rmask.b)
        sch.op(P, lambda e: e.memset(self.rmask.ap[:].rearrange("p (c s) -> p c s", s=LCH)[:, :, 0:1], 0.0),
               writes=self.rmask.b)
        sch.op(P, lambda e: e.memset(self.mMN.ap[:], 1.0), writes=self.mMN.b)
        sch.op(P, lambda e: e.affine_select(out=self.mMN.ap[:, 0, :], in_=self.mMN.ap[:, 0, :], pattern=[[1, 64]],
                                            compare_op=ALU.is_ge, fill=0.0, base=-1, channel_multiplier=-1),
               reads=self.mMN.b, writes=self.mMN.b)
        sch.op(P, lambda e: e.affine_select(out=self.mMN.ap[:, 1, :], in_=self.mMN.ap[:, 1, :], pattern=[[-1, 64]],
                                            compare_op=ALU.is_ge, fill=0.0, base=-1, channel_multiplier=1),
               reads=self.mMN.b, writes=self.mMN.b)
        sch.op(P, lambda e: e.memset(self.mAT.ap[:], 1.0), writes=self.mAT.b)
        for q in range(2):
            for blk in range(2):
                rows = slice(blk * 64, blk * 64 + 64)
                if blk == q:
                    sch.op(P, lambda e, q=q, rows=rows: e.affine_select(
                        out=self.mAT.ap[rows, q, 0:64], in_=self.mAT.ap[rows, q, 0:64], pattern=[[1, 64]],
                        compare_op=ALU.is_ge, fill=0.0, base=-1, channel_multiplier=-1),
                        reads=self.mAT.b, writes=self.mAT.b)
                else:
                    sch.op(P, lambda e, q=q, rows=rows: e.memset(self.mAT.ap[rows, q, 0:64], 0.0), writes=self.mAT.b)
                sch.op(P, lambda e, q=q, rows=rows: e.affine_select(
                    out=self.mAT.ap[rows, q, 64:128], in_=self.mAT.ap[rows, q, 64:128], pattern=[[1, 64]],
                    compare_op=ALU.is_ge, fill=0.0, base=0, channel_multiplier=-1),
                    reads=self.mAT.b, writes=self.mAT.b)

    def rwkv_r1(self, W, src, sname, D):
        sch = self.sch
        self.phase_begin(reset=True)
        wrkv = W["w_rkv"][0].rearrange("(kc p) n -> p kc n", p=128)
        wr = self.wload(wrkv[:, :, 0:C], KC, C)
        wk = self.wload(wrkv[:, :, C:2 * C], KC, C)
        wv = self.wload(wrkv[:, :, 2 * C:3 * C], KC, C)
        w1 = self.wload(W["w1"][0].rearrange("(kc p) n -> p kc n", p=128), KC, 64)
        a1 = self.wload(W["a1"][0].rearrange("(kc p) n -> p kc n", p=128), KC, 64)
        g1 = self.wload(W["g1"][0].rearrange("(kc p) n -> p kc n", p=128), KC, 160)
        w2 = self.wload(W["w2"][0].rearrange("(o p) n -> p o n", o=1), 1, C, part=64)
        a2 = self.wload(W["a2"][0].rearrange("(o p) n -> p o n", o=1), 1, C, part=64)
        g2a = self.wload(W["g2"][0][0:128, :].rearrange("(o p) n -> p o n", o=1), 1, C)
        g2b = self.wload(W["g2"][0][128:160, :].rearrange("(o p) n -> p o n", o=1), 1, C, part=32)
        hx = self.r16([KC, G + 4])
        xn = self.r16([NT, C])
        dx = T(xn.ap[:].rearrange("p t (a b) -> p (t a) b", b=G), xn.b)
        Ls = [self.r16([KC, G]) for _ in range(2)]
        t1 = self.r16([G], part=64)
        a1o = self.r16([G], part=64)
        sgT = self.r16([2, G])
        vtm = self.r16([NT, C])
        gtm = self.r16([NT, C])
        rkT = self.r16([KC, G])
        sq = T(rkT.ap[:, 0:2, :].rearrange("p a b -> p (a b)"), rkT.b[0:1])
        ARs = [self.r16([NCH, 128]) for _ in range(2)]
        BKs = [self.r16([NCH, 128]) for _ in range(2)]
        ksq = self.r16([G])
        SG, CUM, GE, IGE, GX, A_, KK, KP, X1 = [self.a32([G]) for _ in range(9)]
        GLs = self.a32([KC, NCH])
        bon = self.a32([NT, 16])
        mu0 = COLS["mu"]
        cols = self.cols
        gi0 = self.gi
        sch.op("pool", lambda e: e.memset(hx.ap[:, :, 0:1], 0.0), writes=hx.b)
        self.load_x(src, sname, 0, gi0 % 2)

        def lerp(j, L):
            for c in range(KC):
                sch.op("dve", lambda e, c=c, j=j, L=L: e.scalar_tensor_tensor(
                    out=L.ap[:, c, :], in0=dx.ap[:, c, :], scalar=cols[:, mu0 + j * 8 + c:mu0 + j * 8 + c + 1],
                    in1=hx.ap[:, c, 1:G + 1], op0=ALU.mult, op1=ALU.add),
                    reads=dx.b + hx.b + [self.bcols], writes=L.b)

        for g in range(self.NG):
            slot = (gi0 + g) % 2
            if g + 1 < self.NG:
                self.load_x(src, sname, g + 1, (gi0 + g + 1) % 2)
            if g > 0:
                sch.op("pool", lambda e: e.tensor_copy(out=hx.ap[:, :, 0:1], in_=hx.ap[:, :, G:G + 1]),
                       reads=hx.b, writes=hx.b)
            self.norm_T(slot, hx, xn, sq, COLS["mix_norm0"], col0=1)
            sch.op("dve", lambda e: e.tensor_tensor(out=dx.ap[:], in0=hx.ap[:, :, 0:G], in1=hx.ap[:, :, 1:G + 1],
                                                    op=ALU.subtract), reads=hx.b, writes=dx.b)
            lerp(1, Ls[0])
            bk = self.bank[0]
            for k in range(KC):
                sch.op("pe", lambda e, k=k, bk=bk: e.matmul(bk.ap[0:64, :], lhsT=w1.v[:, k, :], rhs=Ls[0].ap[:, k, :],
                                                            start=(k == 0), stop=(k == KC - 1)),
                       reads=Ls[0].b + w1.all(), writes=bk.b)
            sch.op("act", lambda e, bk=bk: e.activation(out=t1.ap[:], in_=bk.ap[0:64, :], func=AF.Tanh),
                   reads=bk.b, writes=t1.b)
            lerp(4, Ls[1])
            bk = self.bank[1]
            for k in range(KC):
                sch.op("pe", lambda e, k=k, bk=bk: e.matmul(bk.ap[0:64, :], lhsT=a1.v[:, k, :], rhs=Ls[1].ap[:, k, :],
                                                            start=(k == 0), stop=(k == KC - 1)),
                       reads=Ls[1].b + a1.all(), writes=bk.b)
            sch.op("act", lambda e, bk=bk: e.activation(out=a1o.ap[:], in_=bk.ap[0:64, :], func=AF.Copy),
                   reads=bk.b, writes=a1o.b)
            lerp(5, Ls[0])
            for m, (m0, mw) in enumerate([(0, 128), (128, 32)]):
                bk = self.bank[2 + m]
                for k in range(KC):
                    sch.op("pe", lambda e, k=k, bk=bk, m0=m0, mw=mw: e.matmul(
                        bk.ap[0:mw, :], lhsT=g1.v[:, k, m0:m0 + mw], rhs=Ls[0].ap[:, k, :],
                        start=(k == 0), stop=(k == KC - 1)), reads=Ls[0].b + g1.all(), writes=bk.b)
                sch.op("act", lambda e, bk=bk, m=m, mw=mw: e.activation(out=sgT.ap[0:mw, m, :], in_=bk.ap[0:mw, :],
                                                                       func=AF.Sigmoid), reads=bk.b, writes=sgT.b)
            for t in range(NT):
                for nb in range(2):
                    bk = self.bank[(t * 2 + nb) % 2]
                    sch.op("pe", lambda e, t=t, nb=nb, bk=bk: e.matmul(
                        bk.ap[:], lhsT=sgT.ap[:, 0, t * 128:(t + 1) * 128], rhs=g2a.v[:, 0, nb * 512:(nb + 1) * 512],
                        start=True, stop=False), reads=sgT.b + g2a.all(), writes=bk.b)
                    sch.op("pe", lambda e, t=t, nb=nb, bk=bk: e.matmul(
                        bk.ap[:], lhsT=sgT.ap[0:32, 1, t * 128:(t + 1) * 128], rhs=g2b.v[:, 0, nb * 512:(nb + 1) * 512],
                        start=False, stop=True), reads=sgT.b + g2b.all(), writes=bk.b)
                    sch.op("act", lambda e, t=t, nb=nb, bk=bk: e.activation(
                        out=gtm.ap[:, t, nb * 512:(nb + 1) * 512], in_=bk.ap[:], func=AF.Copy),
                        reads=bk.b, writes=gtm.b)
            sch.dma("sp", "gst", lambda e, g=g: e.dma_start(
                out=D["Gt"][g * G:(g + 1) * G, :].rearrange("(t p) c -> p t c", p=128), in_=gtm.ap[:]),
                reads=gtm.b, writes=[self.dregion("Gt", g)])
            lerp(3, Ls[1])
            for t in range(NT):
                for nb in range(2):
                    bk = self.bank[2 + (t * 2 + nb) % 2]
                    for k in range(KC):
                        sch.op("pe", lambda e, k=k, t=t, nb=nb, bk=bk: e.matmul(
                            bk.ap[:], lhsT=Ls[1].ap[:, k, t * 128:(t + 1) * 128], rhs=wv.v[:, k, nb * 512:(nb + 1) * 512],
                            start=(k == 0), stop=(k == KC - 1)), reads=Ls[1].b + wv.b(k), writes=bk.b)
                    sch.op("act", lambda e, t=t, nb=nb, bk=bk: e.activation(
                        out=vtm.ap[:, t, nb * 512:(nb + 1) * 512], in_=bk.ap[:], func=AF.Copy),
                        reads=bk.b, writes=vtm.b)
            sch.dma("sp", "vst", lambda e, g=g: e.dma_start(
                out=D["V"][g * G:(g + 1) * G, :].rearrange("(t p) c -> p t c", p=128), in_=vtm.ap[:]),
                reads=vtm.b, writes=[self.dregion("V", g)])
            lerp(0, Ls[0])
            lerp(2, Ls[1])
            for m in range(KC):
                br, bkk, bz, ba, bs = self.bank[m % 2], self.bank[2 + m % 2], self.bank[4], self.bank[5], self.bank[4]
                for k in range(KC):
                    sch.op("pe", lambda e, k=k, m=m, br=br: e.matmul(br.ap[:], lhsT=wr.v[:, k, m * 128:(m + 1) * 128],
                                                             rhs=Ls[0].ap[:, k, :], start=(k == 0), stop=(k == KC - 1)),
                           reads=Ls[0].b + wr.b(k), writes=br.b)
                for k in range(KC):
                    sch.op("pe", lambda e, k=k, m=m, bkk=bkk: e.matmul(bkk.ap[:], lhsT=wk.v[:, k, m * 128:(m + 1) * 128],
                                                             rhs=Ls[1].ap[:, k, :], start=(k == 0), stop=(k == KC - 1)),
                           reads=Ls[1].b + wk.b(k), writes=bkk.b)
                sch.op("pe", lambda e, m=m: e.matmul(bz.ap[:], lhsT=w2.v[:, 0, m * 128:(m + 1) * 128], rhs=t1.ap[:],
                                                     start=True, stop=True), reads=t1.b + w2.all(), writes=bz.b)
                sch.op("pe", lambda e, m=m: e.matmul(ba.ap[:], lhsT=a2.v[:, 0, m * 128:(m + 1) * 128], rhs=a1o.ap[:],
                                                     start=True, stop=True), reads=a1o.b + a2.all(), writes=ba.b)
                cw0, ca0 = COLS["w0"] + m, COLS["a0"] + m
                ckk, cka, com, crk = COLS["k_k"] + m, COLS["k_a"] + m, COLS["omka"] + m, COLS["r_k"] + m
                bc = [self.bcols]
                sch.op("act", lambda e, cw0=cw0: e.activation(out=SG.ap[:], in_=bz.ap[:], func=AF.Sigmoid,
                                                              bias=cols[:, cw0:cw0 + 1]), reads=bz.b + bc, writes=SG.b)
                sch.op("act", lambda e, ca0=ca0: e.activation(out=A_.ap[:], in_=ba.ap[:], func=AF.Sigmoid,
                                                              bias=cols[:, ca0:ca0 + 1]), reads=ba.b + bc, writes=A_.b)
                sch.op("dve", lambda e: e.tensor_tensor_scan(out=CUM.ap[:], data0=self.rmask.ap[:], data1=SG.ap[:],
                                                             initial=0.0, op0=ALU.mult, op1=ALU.add),
                       reads=self.rmask.b + SG.b, writes=CUM.b)
                sch.op("act", lambda e: e.activation(out=GE.ap[:], in_=CUM.ap[:], func=AF.Exp, scale=-DEC),
                       reads=CUM.b, writes=GE.b)
                sch.op("act", lambda e: e.activation(out=IGE.ap[:], in_=CUM.ap[:], func=AF.Exp, scale=DEC),
                       reads=CUM.b, writes=IGE.b)
                sch.op("dve", lambda e: e.tensor_tensor(out=X1.ap[:], in0=CUM.ap[:], in1=SG.ap[:], op=ALU.subtract),
                       reads=CUM.b + SG.b, writes=X1.b)
                sch.op("act", lambda e: e.activation(out=GX.ap[:], in_=X1.ap[:], func=AF.Exp, scale=-DEC),
                       reads=X1.b, writes=GX.b)
                sch.op("dve", lambda e, ckk=ckk, bkk=bkk: e.tensor_scalar(out=KK.ap[:], in0=bkk.ap[:], scalar1=cols[:, ckk:ckk + 1],
                                                                 scalar2=None, op0=ALU.mult), reads=bkk.b + bc, writes=KK.b)
                sch.op("act", lambda e: e.activation(out=ksq.ap[:], in_=KK.ap[:], func=AF.Square),
                       reads=KK.b, writes=ksq.b)
                sch.op("pe", lambda e: e.matmul(bs.ap[:], lhsT=self.oblk.ap[:], rhs=ksq.ap[:], start=True, stop=True),
                       reads=self.oblk.b + ksq.b, writes=bs.b)
                sch.op("act", lambda e: e.activation(out=X1.ap[:], in_=bs.ap[:], func=AF.Sqrt, bias=1e-24),
                       reads=bs.b, writes=X1.b)
                sch.op("dve", lambda e: e.reciprocal(out=X1.ap[:], in_=X1.ap[:]), reads=X1.b, writes=X1.b)
                sch.op("dve", lambda e: e.tensor_tensor(out=KK.ap[:], in0=KK.ap[:], in1=X1.ap[:], op=ALU.mult),
                       reads=KK.b + X1.b, writes=KK.b)
                AR, BK = ARs[m % 2], BKs[m % 2]
                c3 = lambda t_: t_.ap[:].rearrange("p (c s) -> p c s", s=LCH)
                sch.op("dve", lambda e, AR=AR: e.scalar_tensor_tensor(out=AR.ap[:, :, 0:64], in0=c3(KK), scalar=-1.0,
                                                                      in1=c3(GX), op0=ALU.mult, op1=ALU.mult),
                       reads=KK.b + GX.b, writes=AR.b)
                sch.op("dve", lambda e, AR=AR, br=br: e.tensor_tensor(out=AR.ap[:, :, 64:128],
                                                               in0=br.ap[:].rearrange("p (c s) -> p c s", s=LCH),
                                                               in1=c3(GE), op=ALU.mult), reads=br.b + GE.b, writes=AR.b)
                sch.op("dve", lambda e: e.tensor_tensor(out=X1.ap[:], in0=KK.ap[:], in1=A_.ap[:], op=ALU.mult),
                       reads=KK.b + A_.b, writes=X1.b)
                for q in range(2):
                    sch.op("dve", lambda e, q=q, BK=BK: e.tensor_tensor(
                        out=BK.ap[:, q::2, (1 - q) * 64:(1 - q) * 64 + 64], in0=c3(X1)[:, q::2, :], in1=c3(IGE)[:, q::2, :],
                        op=ALU.mult), reads=X1.b + IGE.b, writes=BK.b)
                sch.op("dve", lambda e, cka=cka, com=com: e.tensor_scalar(
                    out=KP.ap[:], in0=A_.ap[:], scalar1=cols[:, cka:cka + 1], scalar2=cols[:, com:com + 1],
                    op0=ALU.mult, op1=ALU.add), reads=A_.b + bc, writes=KP.b)
                sch.op("dve", lambda e, bkk=bkk: e.tensor_tensor(out=KP.ap[:], in0=bkk.ap[:], in1=KP.ap[:], op=ALU.mult),
                       reads=bkk.b + KP.b, writes=KP.b)
                for q in range(2):
                    sch.op("pool" if q == 0 else "dve", lambda e, q=q, BK=BK: e.tensor_tensor(
                        out=BK.ap[:, q::2, q * 64:q * 64 + 64], in0=c3(KP)[:, q::2, :], in1=c3(IGE)[:, q::2, :],
                        op=ALU.mult), reads=KP.b + IGE.b, writes=BK.b)
                sch.op("dve", lambda e, br=br: e.tensor_tensor(out=X1.ap[:], in0=br.ap[:], in1=KP.ap[:], op=ALU.mult),
                       reads=br.b + KP.b, writes=X1.b)
                sch.op("act", lambda e, m=m, crk=crk: e.activation(out=rkT.ap[:, m, :], in_=X1.ap[:], func=AF.Copy,
                                                                   scale=cols[:, crk:crk + 1]),
                       reads=X1.b + bc, writes=rkT.b)
                sch.op("act", lambda e, m=m: e.activation(out=GLs.ap[:, m, :], in_=GE.ap[:, LCH - 1::LCH], func=AF.Copy),
                       reads=GE.b, writes=GLs.b)
                sch.dma("sp", "arst%d" % (m % 2), lambda e, g=g, m=m, AR=AR: e.dma_start(out=D["AR"][g, :, m], in_=AR.ap[:]),
                        reads=AR.b, writes=[self.dregion("AR", g)])
                sch.dma("sp", "bkst%d" % (m % 2), lambda e, g=g, m=m, BK=BK: e.dma_start(out=D["BK"][g, :, m], in_=BK.ap[:]),
                        reads=BK.b, writes=[self.dregion("BK", g)])
            sch.dma("sp", "glst", lambda e, g=g: e.dma_start(out=D["GL"][g], in_=GLs.ap[:]),
                    reads=GLs.b, writes=[self.dregion("GL", g)])
            bb = self.bank[5]
            for t in range(NT):
                for m in range(KC):
                    sch.op("pe", lambda e, t=t, m=m: e.matmul(bb.ap[:, t * 16:(t + 1) * 16],
                                                             lhsT=rkT.ap[:, m, t * 128:(t + 1) * 128],
                                                             rhs=self.hind.ap[:, m, :], start=(m == 0), stop=(m == KC - 1)),
                           reads=rkT.b + self.hind.b, writes=bb.b)
            sch.op("act", lambda e: e.activation(out=bon.ap[:].rearrange("p t h -> p (t h)"), in_=bb.ap[:, 0:NT * 16],
                                                 func=AF.Copy), reads=bb.b, writes=bon.b)
            sch.dma("sp", "bonst", lambda e, g=g: e.dma_start(
                out=D["BON"][g * G:(g + 1) * G, :].rearrange("(t p) h -> p t h", p=128), in_=bon.ap[:]),
                reads=bon.b, writes=[self.dregion("BON", g)])
        self.gi += self.NG

    def rwkv_r2(self, w_o_d, lnw_d, lnb_d, D, base, bname, dst, dname):
        sch = self.sch
        self.phase_begin(reset=True)
        wo = self.wload(w_o_d[0].rearrange("(kc p) n -> p kc n", p=128), KC, C)
        ARg = [self.r16([KC, NCH, 128]) for _ in range(2)]
        BKg = [self.r16([KC, NCH, 128]) for _ in range(2)]
        UV = [self.r16([16, 64]) for _ in range(2)]
        ATs = [self.r16([16, 128]) for _ in range(2)]
        Ms = [self.r16([16, 64], part=64) for _ in range(2)]
        Ns = [self.r16([16, 64], part=64) for _ in range(2)]
        PTs = [self.r16([16, 64], part=64) for _ in range(2)]
        Wsb = self.r16([16, 64], part=64)
        BKtm = self.r16([KC, 128])
        Hb = self.r16([KC, 128])
        yg = self.r16([C])
        ygT = self.r16([KC, G])
        gts = [self.r16([C]) for _ in range(2)]
        ysq = self.r16([16, 64])
        Hf = self.a32([KC, 128])
        ysb = self.a32([16, 64])
        lnw = self.a32([16, 64])
        lnb = self.a32([16, 64])
        GLg = [self.a32([KC, NCH]) for _ in range(2)]
        bons = [self.a32([16]) for _ in range(2)]
        st = self.st
        idf64 = self.ident_f[0:64, 0:64]
        sch.dma("sp", "lnw", lambda e: e.dma_start(out=lnw.ap[:].rearrange("p h i -> p (h i)"), in_=lnw_d[:, :]), writes=lnw.b)
        sch.dma("sp", "lnb", lambda e: e.dma_start(out=lnb.ap[:].rearrange("p h i -> p (h i)"), in_=lnb_d[:, :]), writes=lnb.b)
        sch.op("pool", lambda e: e.memset(Hf.ap[:], 0.0), writes=Hf.b)
        sch.op("pool", lambda e: e.memset(Hb.ap[:], 0.0), writes=Hb.b)
        for q in range(2):
            sch.op("pool", lambda e, q=q: e.memset(UV[q].ap[:], 0.0), writes=UV[q].b)
        gi0 = self.gi
        bif = [self.buf("ident_f")]
        mAT, mMN, bdm = self.mAT, self.mMN, self.bdm
        bank = self.bank
        NG = self.NG

        def prefetch(g):
            sl = (gi0 + g) % 2
            self.load_x(base, bname, g, sl)
            sch.dma("sp", "arld%d" % sl, lambda e, o=ARg[sl].ap[:], i=D["AR"][g]: e.dma_start(out=o, in_=i),
                    reads=[self.dregion("AR", g)], writes=ARg[sl].b)
            sch.dma("sp", "bkld%d" % sl, lambda e, o=BKg[sl].ap[:], i=D["BK"][g]: e.dma_start(out=o, in_=i),
                    reads=[self.dregion("BK", g)], writes=BKg[sl].b)
            sch.dma("sp", "glld%d" % sl, lambda e, o=GLg[sl].ap[:], i=D["GL"][g]: e.dma_start(out=o, in_=i),
                    reads=[self.dregion("GL", g)], writes=GLg[sl].b)

        def idx(h):
            return (h % 2) * 8 + h // 2

        def inv_steps(gc):
            g, ch = gc // NCH, gc % NCH
            q = ch % 2
            slot = (gi0 + g) % 2
            AR_, BK_ = ARg[slot], BKg[slot]
            AT, PT = ATs[gc % 2], PTs[gc % 2]
            bb_ = (1 - q) * 64
            steps = []

            def s_at():
                for h in range(16):
                    pr, po = h // 2, (h % 2) * 64
                    bk = bank[(h % 2) * 2 + pr // 4]
                    cs = (pr % 4) * 128
                    sch.op("pe", lambda e, pr=pr, po=po, bk=bk, cs=cs: e.matmul(
                        bk.ap[:, cs:cs + 128], lhsT=BK_.ap[po:po + 64, pr, ch, :],
                        rhs=AR_.ap[po:po + 64, pr, ch, :], start=True, stop=True),
                        reads=BK_.b + AR_.b, writes=bk.b)
                for b4 in range(4):
                    i0 = (b4 // 2) * 8 + (b4 % 2) * 4
                    sch.op("dve", lambda e, b4=b4, i0=i0: e.tensor_tensor(
                        out=AT.ap[:, i0:i0 + 4, :], in0=bank[b4].ap[:].rearrange("p (h c) -> p h c", h=4),
                        in1=mAT.ap[:, q, :].unsqueeze(1).broadcast_to([128, 4, 128]), op=ALU.mult),
                        reads=bank[b4].b + mAT.b, writes=AT.b)

            def s_nm():
                for which in range(2):
                    for h in range(16):
                        pr, po, i_ = h // 2, (h % 2) * 64, idx(h)
                        bk = bank[2 * which + i_ // 8]
                        a_op = AR_.ap[po:po + 64, pr, ch, 0:64]
                        b_op = BK_.ap[po:po + 64, pr, ch, bb_:bb_ + 64]
                        l_, r_ = (a_op, b_op) if which == 0 else (b_op, a_op)
                        sch.op("pe", lambda e, i_=i_, bk=bk, l_=l_, r_=r_: e.matmul(
                            bk.ap[0:64, (i_ % 8) * 64:(i_ % 8 + 1) * 64], lhsT=l_, rhs=r_, start=True, stop=True),
                            reads=BK_.b + AR_.b, writes=bk.b)
                for which in range(2):
                    dstT = Ns[0] if which == 0 else Ms[0]
                    mk = mMN.ap[:, 1, :] if which == 0 else mMN.ap[:, 0, :]
                    for b2 in range(2):
                        bk = bank[2 * which + b2]
                        sch.op("dve", lambda e, b2=b2, dstT=dstT, mk=mk, bk=bk: e.tensor_tensor(
                            out=dstT.ap[:, b2 * 8:(b2 + 1) * 8, :],
                            in0=bk.ap[0:64, :].rearrange("p (h c) -> p h c", h=8),
                            in1=mk.unsqueeze(1).broadcast_to([64, 8, 64]), op=ALU.mult),
                            reads=bk.b + mMN.b, writes=dstT.b)
                sch.op("pool", lambda e: e.tensor_tensor(out=PT.ap[:], in0=Ms[0].ap[:],
                                                         in1=idf64.unsqueeze(1).broadcast_to([64, 16, 64]), op=ALU.add),
                       reads=Ms[0].b + bif, writes=PT.b)

            def s_sq(lev):
                cur = lev % 2
                Mo, No, Mn, Nn = Ms[cur], Ns[cur], Ms[1 - cur], Ns[1 - cur]
                last = lev == 4
                for i_ in range(16):
                    if not last:
                        sch.op("pe", lambda e, i_=i_: e.matmul(
                            bank[i_ // 8].ap[0:64, (i_ % 8) * 64:(i_ % 8 + 1) * 64], lhsT=No.ap[:, i_, :], rhs=Mo.ap[:, i_, :],
                            start=True, stop=True), reads=Mo.b + No.b, writes=bank[i_ // 8].b)
                    sch.op("pe", lambda e, i_=i_: e.matmul(
                        bank[2 + i_ // 8].ap[0:64, (i_ % 8) * 64:(i_ % 8 + 1) * 64], lhsT=Mo.ap[:, i_, :], rhs=No.ap[:, i_, :],
                        start=True, stop=True), reads=Mo.b + No.b, writes=bank[2 + i_ // 8].b)
                for b2 in range(2):
                    if not last:
                        sch.op("act", lambda e, b2=b2: e.activation(
                            out=Mn.ap[:, b2 * 8:(b2 + 1) * 8, :].rearrange("p h c -> p (h c)"), in_=bank[b2].ap[0:64, :],
                            func=AF.Copy), reads=bank[b2].b, writes=Mn.b)
                    sch.op("dve", lambda e, b2=b2: e.tensor_copy(
                        out=Nn.ap[:, b2 * 8:(b2 + 1) * 8, :].rearrange("p h c -> p (h c)"), in_=bank[2 + b2].ap[0:64, :]),
                        reads=bank[2 + b2].b, writes=Nn.b)

            def s_pt(lev):
                Nn = Ns[1 - lev % 2]
                for i_ in range(16):
                    sch.op("pe", lambda e, i_=i_: e.matmul(
                        bank[2 + i_ // 8].ap[0:64, (i_ % 8) * 64:(i_ % 8 + 1) * 64], lhsT=Nn.ap[:, i_, :], rhs=PT.ap[:, i_, :],
                        start=True, stop=True), reads=Nn.b + PT.b, writes=bank[2 + i_ // 8].b)
                for b2 in range(2):
                    sch.op("dve", lambda e, b2=b2: e.tensor_tensor(
                        out=PT.ap[:, b2 * 8:(b2 + 1) * 8, :].rearrange("p h c -> p (h c)"), in0=bank[2 + b2].ap[0:64, :],
                        in1=PT.ap[:, b2 * 8:(b2 + 1) * 8, :].rearrange("p h c -> p (h c)"), op=ALU.add),
                        reads=bank[2 + b2].b + PT.b, writes=PT.b)

            steps.append(s_at)
            steps.append(s_nm)
            for lev in range(5):
                steps.append(lambda lev=lev: s_sq(lev))
                steps.append(lambda lev=lev: s_pt(lev))
            return steps

        carry = {}

        def state_steps(gc):
            g, ch = gc // NCH, gc % NCH
            q, tt = ch % 2, ch // 2
            slot = (gi0 + g) % 2
            AR_, BK_, GL_ = ARg[slot], BKg[slot], GLg[slot]
            AT, PT = ATs[gc % 2], PTs[gc % 2]
            uv = UV[q]
            kb, bb_ = q * 64, (1 - q) * 64
            tok0 = g * G + ch * LCH
            steps = []

            def s_load_w():
                sch.dma("sp", "vld%d" % q, lambda e: e.dma_start(
                    out=uv.ap[q * 64:q * 64 + 64, :, :].rearrange("p h i -> p (h i)"), in_=D["V"][tok0:tok0 + LCH, :]),
                    reads=[self.dregion("V", g)], writes=uv.b)
                if q == 0:
                    gt, bo_ = gts[tt % 2], bons[tt % 2]
                    carry["gt"], carry["bo"] = gt, bo_
                    sch.dma("sp", "gld%d" % (tt % 2), lambda e: e.dma_start(
                        out=gt.ap[:], in_=D["Gt"][tok0:tok0 + 128, :]), reads=[self.dregion("Gt", g)], writes=gt.b)
                    sch.dma("sp", "bld%d" % (tt % 2), lambda e: e.dma_start(
                        out=bo_.ap[:], in_=D["BON"][tok0:tok0 + 128, :]), reads=[self.dregion("BON", g)], writes=bo_.b)
                for pr in range(KC):
                    bk = bank[4 + pr // 4]
                    c0 = (pr % 4) * 128
                    sch.op("pe", lambda e, pr=pr, bk=bk, c0=c0: e.matmul(
                        bk.ap[0:64, c0:c0 + 128], lhsT=AR_.ap[:, pr, ch, 0:64], rhs=Hb.ap[:, pr, :],
                        start=True, stop=False), reads=AR_.b + Hb.b, writes=bk.b)
                    for hp in range(2):
                        h = 2 * pr + hp
                        sch.op("pe", lambda e, h=h, hp=hp, bk=bk, c0=c0: e.matmul(
                            bk.ap[0:64, c0 + hp * 64:c0 + hp * 64 + 64], lhsT=AT.ap[:, idx(h), 0:64], rhs=uv.ap[:, h, :],
                            start=False, stop=(hp == 1)), reads=AT.b + uv.b, writes=bk.b)
                for b2 in range(2):
                    sch.op("act", lambda e, b2=b2: e.activation(
                        out=Wsb.ap[:, b2 * 8:(b2 + 1) * 8, :].rearrange("p h c -> p (h c)"), in_=bank[4 + b2].ap[0:64, :],
                        func=AF.Copy), reads=bank[4 + b2].b, writes=Wsb.b)

            def s_u():
                for h in range(16):
                    bk = bank[4 + h // 8]
                    sch.op("pe", lambda e, h=h, bk=bk: e.matmul(
                        bk.ap[bb_:bb_ + 64, (h % 8) * 64:(h % 8 + 1) * 64], lhsT=PT.ap[:, idx(h), :], rhs=Wsb.ap[:, h, :],
                        start=True, stop=True), reads=PT.b + Wsb.b, writes=bk.b)
                for b2 in range(2):
                    sch.op("dve", lambda e, b2=b2: e.tensor_copy(
                        out=uv.ap[bb_:bb_ + 64, b2 * 8:(b2 + 1) * 8, :].rearrange("p h c -> p (h c)"),
                        in_=bank[4 + b2].ap[bb_:bb_ + 64, :]), reads=bank[4 + b2].b, writes=uv.b)

            def s_y():
                for pr in range(KC):
                    bk = bank[4 + pr // 4]
                    c0 = (pr % 4) * 128
                    sch.op("pe", lambda e, pr=pr, bk=bk, c0=c0: e.matmul(
                        bk.ap[kb:kb + 64, c0:c0 + 128], lhsT=AR_.ap[:, pr, ch, 64:128], rhs=Hb.ap[:, pr, :],
                        start=True, stop=False), reads=AR_.b + Hb.b, writes=bk.b)
                    for hp in range(2):
                        h = 2 * pr + hp
                        sch.op("pe", lambda e, h=h, hp=hp, bk=bk, c0=c0: e.matmul(
                            bk.ap[kb:kb + 64, c0 + hp * 64:c0 + hp * 64 + 64], lhsT=AT.ap[:, idx(h), 64:128],
                            rhs=uv.ap[:, h, :], start=False, stop=(hp == 1)), reads=AT.b + uv.b, writes=bk.b)
                for b2 in range(2):
                    sch.op("act", lambda e, b2=b2: e.activation(
                        out=ysb.ap[kb:kb + 64, b2 * 8:(b2 + 1) * 8, :].rearrange("p h c -> p (h c)"),
                        in_=bank[4 + b2].ap[kb:kb + 64, :], func=AF.Copy), reads=bank[4 + b2].b, writes=ysb.b)
                for pr in range(KC):
                    sch.op("pe", lambda e, pr=pr: e.transpose(
                        out=self.bankT[pr // 4].ap[:, (pr % 4) * 128:(pr % 4 + 1) * 128], in_=BK_.ap[:, pr, ch, :],
                        identity=self.ident[:]), reads=BK_.b + [self.bident], writes=self.bankT[pr // 4].b)
                for b2 in range(2):
                    sch.op("act", lambda e, b2=b2: e.activation(
                        out=BKtm.ap[:, b2 * 4:(b2 + 1) * 4, :].rearrange("p h c -> p (h c)"), in_=self.bankT[b2].ap[:, 0:512],
                        func=AF.Copy), reads=self.bankT[b2].b, writes=BKtm.b)

            def s_state():
                for b2 in range(2):
                    bk = bank[4 + b2]
                    for p4 in range(4):
                        pr = b2 * 4 + p4
                        sch.op("pe", lambda e, pr=pr, p4=p4, bk=bk: e.matmul(
                            bk.ap[:, p4 * 128:(p4 + 1) * 128], lhsT=BKtm.ap[:, pr, :],
                            rhs=uv.ap[:, 2 * pr:2 * pr + 2, :].rearrange("p h c -> p (h c)"), start=True, stop=True),
                            reads=BKtm.b + uv.b, writes=bk.b)
                    hs = slice(b2 * 4, b2 * 4 + 4)
                    sch.op("dve", lambda e, bk=bk: e.tensor_tensor(
                        out=bk.ap[:].rearrange("p (h c) -> p h c", h=4), in0=bk.ap[:].rearrange("p (h c) -> p h c", h=4),
                        in1=bdm.ap[:].unsqueeze(1).broadcast_to([128, 4, 128]), op=ALU.mult),
                        reads=bk.b + bdm.b, writes=bk.b)
                    sch.op("dve", lambda e, hs=hs, bk=bk: e.tensor_tensor(
                        out=Hf.ap[:, hs, :].rearrange("p h c -> p (h c)"), in0=bk.ap[:],
                        in1=Hf.ap[:, hs, :].rearrange("p h c -> p (h c)"), op=ALU.add), reads=bk.b + Hf.b, writes=Hf.b)
                    sch.op("dve", lambda e, hs=hs: e.tensor_tensor(
                        out=Hf.ap[:, hs, :], in0=Hf.ap[:, hs, :], in1=GL_.ap[:, hs, ch:ch + 1].broadcast_to([128, 4, 128]),
                        op=ALU.mult), reads=Hf.b + GL_.b, writes=Hf.b)
                    sch.op("act", lambda e, hs=hs: e.activation(out=Hb.ap[:, hs, :], in_=Hf.ap[:, hs, :], func=AF.Copy),
                           reads=Hf.b, writes=Hb.b)

            def s_post1():
                sch.op("act", lambda e: e.activation(out=ysq.ap[:], in_=ysb.ap[:], func=AF.Square), reads=ysb.b, writes=ysq.b)
                sch.op("dve", lambda e: e.tensor_reduce(out=st.ap[:, 0, :], in_=ysb.ap[:], axis=AX.X, op=ALU.add),
                       reads=ysb.b, writes=st.b)
                sch.op("dve", lambda e: e.tensor_reduce(out=st.ap[:, 1, :], in_=ysq.ap[:], axis=AX.X, op=ALU.add),
                       reads=ysq.b, writes=st.b)
                sch.op("dve", lambda e: e.tensor_scalar(out=st.ap[:, 2, :], in0=st.ap[:, 0, :], scalar1=1.0 / 64, scalar2=None,
                                                        op0=ALU.mult), reads=st.b, writes=st.b)
                sch.op("dve", lambda e: e.tensor_tensor(out=st.ap[:, 0, :], in0=st.ap[:, 2, :], in1=st.ap[:, 2, :],
                                                        op=ALU.mult), reads=st.b, writes=st.b)
                sch.op("dve", lambda e: e.scalar_tensor_tensor(out=st.ap[:, 3, :], in0=st.ap[:, 1, :], scalar=1.0 / 64,
                                                               in1=st.ap[:, 0, :], op0=ALU.mult, op1=ALU.subtract),
                       reads=st.b, writes=st.b)
                sch.op("act", lambda e: e.activation(out=st.ap[:, 3, :], in_=st.ap[:, 3, :], func=AF.Sqrt, bias=float(GN_EPS)),
                       reads=st.b, writes=st.b)
                sch.op("dve", lambda e: e.reciprocal(out=st.ap[:, 3, :], in_=st.ap[:, 3, :]), reads=st.b, writes=st.b)

            def s_post2():
                gt, bo_ = carry["gt"], carry["bo"]
                b16 = lambda a_: a_.unsqueeze(2).broadcast_to([128, 16, 64])
                sch.op("dve", lambda e: e.tensor_tensor(out=ysb.ap[:], in0=ysb.ap[:], in1=b16(st.ap[:, 2, :]), op=ALU.subtract),
                       reads=ysb.b + st.b, writes=ysb.b)
                sch.op("dve", lambda e: e.tensor_tensor(out=ysb.ap[:], in0=ysb.ap[:], in1=b16(st.ap[:, 3, :]), op=ALU.mult),
                       reads=ysb.b + st.b, writes=ysb.b)
                sch.op("pool", lambda e: e.tensor_tensor(out=ysb.ap[:], in0=ysb.ap[:], in1=lnw.ap[:], op=ALU.mult),
                       reads=ysb.b + lnw.b, writes=ysb.b)
                sch.op("pool", lambda e: e.tensor_tensor(out=ysb.ap[:], in0=ysb.ap[:], in1=lnb.ap[:], op=ALU.add),
                       reads=ysb.b + lnb.b, writes=ysb.b)
                for qq in range(2):
                    rows = slice(qq * 64, qq * 64 + 64)
                    sch.op("pool", lambda e, qq=qq, rows=rows: e.tensor_tensor(
                        out=ysq.ap[rows, :, :], in0=UV[qq].ap[rows, :, :],
                        in1=bo_.ap[rows, :].unsqueeze(2).broadcast_to([64, 16, 64]), op=ALU.mult),
                        reads=UV[qq].b + bo_.b, writes=ysq.b)
                sch.op("dve", lambda e: e.tensor_tensor(out=ysb.ap[:], in0=ysb.ap[:], in1=ysq.ap[:], op=ALU.add),
                       reads=ysb.b + ysq.b, writes=ysb.b)
                sch.op("dve", lambda e: e.tensor_tensor(out=yg.ap[:], in0=ysb.ap[:].rearrange("p h c -> p (h c)"),
                                                        in1=gt.ap[:], op=ALU.mult), reads=ysb.b + gt.b, writes=yg.b)
                for c in range(KC):
                    sch.op("pe", lambda e, c=c: e.transpose(
                        out=self.bankT[c // 4].ap[:, (c % 4) * 128:(c % 4 + 1) * 128], in_=yg.ap[:, c * 128:(c + 1) * 128],
                        identity=self.ident[:]), reads=yg.b + [self.bident], writes=self.bankT[c // 4].b)
                for b2 in range(2):
                    sch.op("act", lambda e, b2=b2: e.activation(
                        out=ygT.ap[:, b2 * 4:(b2 + 1) * 4, tt * 128:(tt + 1) * 128],
                        in_=self.bankT[b2].ap[:, 0:512].rearrange("p (c t) -> p c t", c=4), func=AF.Copy),
                        reads=self.bankT[b2].b, writes=ygT.b)

            def s_group_end():
                self.out_proj(ygT, KC, wo, 1.0, slot)
                carry["tok"] = self.store_x(dst, dname, g, slot)

            steps += [s_load_w, s_u, s_y, s_state]
            if q == 1:
                steps += [s_post1, s_post2]
            if ch == NCH - 1:
                steps.append(s_group_end)
            return steps

        prefetch(0)
        ntot = NG * NCH
        for st_ in inv_steps(0):
            st_()
        for gc in range(ntot):
            g, ch = gc // NCH, gc % NCH
            if ch == 0 and g + 1 < NG:
                prefetch(g + 1)
            A = state_steps(gc)
            Bq = inv_steps(gc + 1) if gc + 1 < ntot else []
            i = j = 0
            while i < len(A) or j < len(Bq):
                for _ in range(2):
                    if j < len(Bq):
                        Bq[j]()
                        j += 1
                if i < len(A):
                    A[i]()
                    i += 1
        self.gi += NG
        return carry.get("tok")


WEIGHT_SHAPES = {
    "mem_w_kv": [C, 2 * C], "ffn1_w_in": [2, C, 2 * DFF], "ffn1_w_out": [2, DFF, C],
    "xattn_wq": [2, C, C], "xattn_wo": [2, C, C], "ffn2_w_in": [2, C, 2 * DFF], "ffn2_w_out": [2, DFF, C],
    "rwkv_w_rkv": [1, C, 3 * C], "rwkv_w1": [1, C, 64], "rwkv_w2": [1, 64, C], "rwkv_a1": [1, C, 64],
    "rwkv_a2": [1, 64, C], "rwkv_g1": [1, C, 160], "rwkv_g2": [1, 160, C], "rwkv_w_o": [1, C, C],
    "gmlp_w_uv": [1, C, 4096], "gmlp_w_o": [1, 2048, C],
    "wsT": [128, 16 * 128], "bsT": [128, 16 * 128], "lnw_bc": [128, C], "lnb_bc": [128, C], "fin_bc": [128, C],
    "mem": [NMEM, C],
}
MODE_INPUTS = {
    "ffn_test": ["ffn1_w_in", "ffn1_w_out"],
    "xattn_test": ["mem", "mem_w_kv", "xattn_wq", "xattn_wo"],
    "gmlp_test": ["gmlp_w_uv", "gmlp_w_o", "wsT", "bsT"],
    "rwkv_test": ["rwkv_w_rkv", "rwkv_w1", "rwkv_w2", "rwkv_a1", "rwkv_a2", "rwkv_g1", "rwkv_g2", "rwkv_w_o",
                  "lnw_bc", "lnb_bc"],
    "final_test": ["fin_bc"],
    "memkv_test": ["fin_bc", "mem", "mem_w_kv"],
}
MODE_INPUTS["full"] = sorted(set(sum([v for k, v in MODE_INPUTS.items() if k != "memkv_test"], [])
                                 + ["ffn2_w_in", "ffn2_w_out"]))


def build_program(S=4096, mode="full"):
    P = Prog(S)
    x = P.din("x", [S, C])
    out = P.dout("out", [S, C])
    P.setup()
    d = {n: P.din(n, WEIGHT_SHAPES[n]) for n in MODE_INPUTS[mode]}
    NG = P.NG
    P.hsave = P.dscratch("hsave", [NG, 128, KC, G], BF16)
    A = (P.dscratch("sA", [S, C]), "A")
    B = (P.dscratch("sB", [S, C]), "B")
    Cc = (P.dscratch("sC", [S, C]), "Cc")
    X = (x, "x")
    O = (out, "out")
    if mode in ("rwkv_test", "full"):
        P.rwkv_consts()
        D = {"AR": P.dscratch("dAR", [NG, 128, KC, NCH, 128], BF16),
             "BK": P.dscratch("dBK", [NG, 128, KC, NCH, 128], BF16),
             "V": P.dscratch("dV", [S, C], BF16), "Gt": P.dscratch("dGt", [S, C], BF16),
             "BON": P.dscratch("dBON", [S, 16]), "GL": P.dscratch("dGL", [NG, 128, KC, NCH])}
        RW = {"w_rkv": d["rwkv_w_rkv"], "w1": d["rwkv_w1"], "w2": d["rwkv_w2"], "a1": d["rwkv_a1"],
              "a2": d["rwkv_a2"], "g1": d["rwkv_g1"], "g2": d["rwkv_g2"]}

    def rwkv(src, dst):
        P.rwkv_r1(RW, src[0], src[1], D)
        return P.rwkv_r2(d["rwkv_w_o"], d["lnw_bc"], d["lnb_bc"], D, src[0], src[1], dst[0], dst[1])

    if mode == "ffn_test":
        P.ffn(d["ffn1_w_in"], d["ffn1_w_out"], 0, x, "x", A, B, out, "out", COLS["ffn1_norm0"])
    elif mode == "xattn_test":
        P.memkv(d["mem"], d["mem_w_kv"])
        P.xattn(d["xattn_wq"], d["xattn_wo"], 0, x, "x", out, "out", COLS["xattn_norm0"])
    elif mode == "gmlp_test":
        P.gmlp(d["gmlp_w_uv"], d["wsT"], d["bsT"], d["gmlp_w_o"], x, "x", out, "out", COLS["mix_norm1"])
    elif mode == "rwkv_test":
        rwkv(X, O)
    elif mode == "final_test":
        P.final_norm(d["fin_bc"], x, "x", out, "out")
    elif mode == "memkv_test":
        P.memkv(d["mem"], d["mem_w_kv"])
        P.final_norm(d["fin_bc"], x, "x", out, "out")
    else:
        P.memkv(d["mem"], d["mem_w_kv"])
        P.ffn(d["ffn1_w_in"], d["ffn1_w_out"], 0, x, "x", A, B, Cc[0], Cc[1], COLS["ffn1_norm0"])
        rwkv(Cc, A)
        P.xattn(d["xattn_wq"], d["xattn_wo"], 0, A[0], A[1], B[0], B[1], COLS["xattn_norm0"])
        P.ffn(d["ffn2_w_in"], d["ffn2_w_out"], 0, B[0], B[1], A, Cc, A[0], A[1], COLS["ffn2_norm0"])
        P.ffn(d["ffn1_w_in"], d["ffn1_w_out"], 1, A[0], A[1], B, Cc, B[0], B[1], COLS["ffn1_norm1"])
        P.gmlp(d["gmlp_w_uv"], d["wsT"], d["bsT"], d["gmlp_w_o"], B[0], B[1], A[0], A[1], COLS["mix_norm1"])
        P.xattn(d["xattn_wq"], d["xattn_wo"], 1, A[0], A[1], B[0], B[1], COLS["xattn_norm1"])
        P.ffn(d["ffn2_w_in"], d["ffn2_w_out"], 1, B[0], B[1], A, Cc, A[0], A[1], COLS["ffn2_norm1"])
        P.final_norm(d["fin_bc"], A[0], A[1], out, "out")
    finals = [s[2] for n, s in P.sch.dma_sems.items() if n.startswith("xst") and s[2] is not None]
    P.sch.run(finals)
    return P


def make_cols(inp):
    cols = np.zeros((128, NCOLS), np.float32)

    def put(name, v):
        v = np.asarray(v, np.float32).reshape(-1, 128)
        cols[:, COLS[name]:COLS[name] + v.shape[0]] = v.T

    for l in range(2):
        put("ffn1_norm%d" % l, inp["ffn1_norm"][l])
        put("mix_norm%d" % l, inp["mix_norm"][l])
        put("xattn_norm%d" % l, inp["xattn_norm"][l])
        put("ffn2_norm%d" % l, inp["ffn2_norm"][l])
    put("mem_norm", inp["mem_norm"])
    put("final_norm", inp["final_norm"])
    put("mu", inp["rwkv_mu"][0])
    put("w0", inp["rwkv_w0"][0])
    put("a0", inp["rwkv_a0"][0])
    put("k_k", inp["rwkv_k_k"][0])
    put("k_a", inp["rwkv_k_a"][0])
    put("r_k", inp["rwkv_r_k"][0])
    put("gv_norm", inp["gmlp_v_norm"][0])
    return cols


def host_layout(inp, names):
    f = lambda a: np.ascontiguousarray(np.asarray(a, np.float32))
    rep = lambda v: f(np.broadcast_to(np.asarray(v, np.float32).reshape(1, -1), (128, np.asarray(v).size)))
    out = {}
    for n in names:
        if n == "wsT":
            out[n] = f(np.transpose(np.asarray(inp["gmlp_w_s"][0], np.float32), (2, 0, 1)).reshape(128, 16 * 128))
        elif n == "bsT":
            out[n] = rep(inp["gmlp_b_s"][0])
        elif n == "lnw_bc":
            out[n] = rep(inp["rwkv_ln_w"][0])
        elif n == "lnb_bc":
            out[n] = rep(inp["rwkv_ln_b"][0])
        elif n == "fin_bc":
            out[n] = rep(inp["final_norm"])
        elif n == "mem":
            continue
        else:
            out[n] = f(inp[n])
    return out


_PROG_CACHE = {}


def kernel(**inputs):
    S = 4096
    if "full" not in _PROG_CACHE:
        _PROG_CACHE["full"] = build_program(S, "full")
    P = _PROG_CACHE["full"]
    shared = host_layout(inputs, MODE_INPUTS["full"])
    shared["cols"] = make_cols(inputs)
    x = np.asarray(inputs["x"], np.float32)
    mem = np.asarray(inputs["mem"], np.float32)
    in_maps = []
    for b in range(8):
        m = dict(shared)
        m["x"] = np.ascontiguousarray(x[b])
        m["mem"] = np.ascontiguousarray(mem[b])
        in_maps.append(m)
    res = run_bass_kernel_spmd(P.nc, in_maps, core_ids=list(range(8)))
    return np.stack([res.results[b]["out"] for b in range(8)], axis=0).astype(np.float32)
```
